# Optimizing a Trainium2 kernel written in Bass

```python
import math
import jax, jax.numpy as jnp
from jax import lax
import numpy as np

D_MODEL = 2048
BATCH = 1
SEQ = 16384
DEPTH = 2

N_MIXERS = 2
N_DIFF_LAYERS = (DEPTH + 1) // 2
N_MLA_LAYERS = DEPTH // 2

DIFF_HEAD_DIM = 128
DIFF_V_DIM = 2 * DIFF_HEAD_DIM
DIFF_HEADS = D_MODEL // DIFF_V_DIM
DIFF_QK_WIDTH = 2 * DIFF_HEADS * DIFF_HEAD_DIM
DIFF_QKV_WIDTH = 2 * DIFF_QK_WIDTH + DIFF_HEADS * DIFF_V_DIM

MLA_HEADS = D_MODEL // 128
MLA_Q_LORA = 512
MLA_KV_LORA = 512
MLA_NOPE_DIM = 128
MLA_ROPE_DIM = 64
MLA_V_DIM = 128
MLA_DOWN_WIDTH = MLA_Q_LORA + MLA_KV_LORA + MLA_ROPE_DIM
ROPE_THETA = 10000.0

def _round_up(n, m):
    return ((n + m - 1) // m) * m
FFN_HIDDEN = _round_up(-(-8 * D_MODEL // 3), 256)

Q_BLOCK = 128
EPS = 1e-6

kernel_name = "hybrid_diffattn_mla_swiglu_sandwich"


def _rmsnorm(x, g):
    xf = x.astype(jnp.float32)
    y = xf * lax.rsqrt(jnp.mean(xf * xf, axis=-1, keepdims=True) + EPS)
    return (y * g.astype(jnp.float32)).astype(x.dtype)


def _seq_blocks(t):
    b, h, s, d = t.shape
    return t.reshape(b, h, s // Q_BLOCK, Q_BLOCK, d).transpose(2, 0, 1, 3, 4)


def _unblock(o):
    nb, b, h, qb, d = o.shape
    return o.transpose(1, 2, 0, 3, 4).reshape(b, h, nb * qb, d)


def _pos_blocks(positions):
    b, s = positions.shape
    return positions.reshape(b, s // Q_BLOCK, Q_BLOCK).transpose(1, 0, 2)


def _causal_mask(blk, s):
    q_idx = blk * Q_BLOCK + jnp.arange(Q_BLOCK)
    k_idx = jnp.arange(s)
    return k_idx[None, :] <= q_idx[:, None]


def _masked_softmax(scores, causal):
    return jax.nn.softmax(jnp.where(causal, scores, -jnp.inf), axis=-1)


def _alibi_slopes(n_heads):
    return jnp.exp2(-8.0 * jnp.arange(1, n_heads + 1, dtype=jnp.float32) / n_heads)


def _rope_tables(positions, dim):
    inv_freq = ROPE_THETA ** (-jnp.arange(0, dim, 2, dtype=jnp.float32) / dim)
    ang = positions.astype(jnp.float32)[..., None] * inv_freq
    return jnp.cos(ang), jnp.sin(ang)


def _apply_rope(x, cos, sin):
    x1, x2 = jnp.split(x, 2, axis=-1)
    cos = cos.astype(x.dtype)
    sin = sin.astype(x.dtype)
    return jnp.concatenate([x1 * cos - x2 * sin, x2 * cos + x1 * sin], axis=-1)


def _diff_attention(h, w_qkv, lam, subln, w_o, positions, layer_idx):
    b, s, _ = h.shape
    H, dk, dv = DIFF_HEADS, DIFF_HEAD_DIM, DIFF_V_DIM
    qkv = h @ w_qkv
    q, k, v = jnp.split(qkv, [DIFF_QK_WIDTH, 2 * DIFF_QK_WIDTH], axis=-1)
    q = q.reshape(b, s, H, 2, dk).transpose(0, 2, 3, 1, 4)
    k = k.reshape(b, s, H, 2, dk).transpose(0, 2, 3, 1, 4)
    v = v.reshape(b, s, H, dv).transpose(0, 2, 1, 3)
    q1, q2 = q[:, :, 0], q[:, :, 1]
    k1, k2 = k[:, :, 0], k[:, :, 1]

    lambda_init = 0.8 - 0.6 * math.exp(-0.3 * layer_idx)
    lf = lam.astype(jnp.float32)
    lam_full = jnp.exp(jnp.sum(lf[0] * lf[1])) - jnp.exp(jnp.sum(lf[2] * lf[3])) + lambda_init
    slopes = _alibi_slopes(H)
    scale = dk ** -0.5
    pos_k = positions.astype(jnp.float32)

    def block(args):
        blk, q1b, q2b, pq = args
        dist = jnp.abs(pq.astype(jnp.float32)[:, :, None] - pos_k[:, None, :])
        bias = -slopes[None, :, None, None] * dist[:, None]
        causal = _causal_mask(blk, s)
        s1 = jnp.einsum('bhqd,bhkd->bhqk', q1b, k1).astype(jnp.float32) * scale + bias
        s2 = jnp.einsum('bhqd,bhkd->bhqk', q2b, k2).astype(jnp.float32) * scale + bias
        p = _masked_softmax(s1, causal) - lam_full * _masked_softmax(s2, causal)
        return jnp.einsum('bhqk,bhkd->bhqd', p.astype(v.dtype), v)

    nb = s // Q_BLOCK
    o = lax.map(block, (jnp.arange(nb), _seq_blocks(q1), _seq_blocks(q2), _pos_blocks(positions)))
    o = _unblock(o)
    o = _rmsnorm(o, subln) * (1.0 - lambda_init)
    o = o.transpose(0, 2, 1, 3).reshape(b, s, H * dv)
    return o @ w_o


def _mla(h, w_down, q_norm, kv_norm, w_uq, w_ukv, w_o, positions):
    b, s, _ = h.shape
    H = MLA_HEADS
    c = h @ w_down
    cq, ckv, k_rope = jnp.split(c, [MLA_Q_LORA, MLA_Q_LORA + MLA_KV_LORA], axis=-1)
    q = (_rmsnorm(cq, q_norm) @ w_uq).reshape(b, s, H, MLA_NOPE_DIM + MLA_ROPE_DIM)
    kv = (_rmsnorm(ckv, kv_norm) @ w_ukv).reshape(b, s, H, MLA_NOPE_DIM + MLA_V_DIM)
    q_nope, q_rope = jnp.split(q, [MLA_NOPE_DIM], axis=-1)
    k_nope, v = jnp.split(kv, [MLA_NOPE_DIM], axis=-1)

    cos, sin = _rope_tables(positions, MLA_ROPE_DIM)
    q_rope = _apply_rope(q_rope, cos[:, :, None], sin[:, :, None])
    k_rope = _apply_rope(k_rope, cos, sin)

    q_nope = q_nope.transpose(0, 2, 1, 3)
    q_rope = q_rope.transpose(0, 2, 1, 3)
    k_nope = k_nope.transpose(0, 2, 1, 3)
    v = v.transpose(0, 2, 1, 3)
    scale = (MLA_NOPE_DIM + MLA_ROPE_DIM) ** -0.5

    def block(args):
        blk, qnb, qrb = args
        causal = _causal_mask(blk, s)
        sc = (jnp.einsum('bhqd,bhkd->bhqk', qnb, k_nope)
              + jnp.einsum('bhqr,bkr->bhqk', qrb, k_rope)).astype(jnp.float32) * scale
        p = _masked_softmax(sc, causal)
        return jnp.einsum('bhqk,bhkd->bhqd', p.astype(v.dtype), v)

    nb = s // Q_BLOCK
    o = lax.map(block, (jnp.arange(nb), _seq_blocks(q_nope), _seq_blocks(q_rope)))
    o = _unblock(o).transpose(0, 2, 1, 3).reshape(b, s, H * MLA_V_DIM)
    return o @ w_o


def _swiglu(h, w_in, w_out):
    g, u = jnp.split(h @ w_in, [FFN_HIDDEN], axis=-1)
    return (jax.nn.silu(g) * u) @ w_out


def setup_inputs(seed: int = 0) -> dict:
    key = jax.random.key(seed)
    ks = jax.random.split(key, 16)
    f32 = jnp.float32

    def dense(k, shape, fan_in):
        return jax.random.normal(k, shape, f32) * fan_in ** -0.5

    def gain(k, shape):
        return 1.0 + 0.02 * jax.random.normal(k, shape, f32)

    x = jax.random.normal(ks[0], (BATCH, SEQ, D_MODEL), f32)
    offset = jax.random.randint(ks[1], (BATCH, 1), 0, 4096, dtype=jnp.int32)
    positions = offset + jnp.arange(SEQ, dtype=jnp.int32)[None, :]
    norm_gains = gain(ks[2], (DEPTH, 4, D_MODEL))

    diff_w_qkv = dense(ks[3], (N_DIFF_LAYERS, D_MODEL, DIFF_QKV_WIDTH), D_MODEL)
    diff_lambda = 0.1 * jax.random.normal(ks[4], (N_DIFF_LAYERS, 4, DIFF_HEAD_DIM), f32)
    diff_subln = gain(ks[5], (N_DIFF_LAYERS, DIFF_V_DIM))
    diff_w_o = dense(ks[6], (N_DIFF_LAYERS, DIFF_HEADS * DIFF_V_DIM, D_MODEL), DIFF_HEADS * DIFF_V_DIM)

    mla_w_down = dense(ks[7], (N_MLA_LAYERS, D_MODEL, MLA_DOWN_WIDTH), D_MODEL)
    mla_q_norm = gain(ks[8], (N_MLA_LAYERS, MLA_Q_LORA))
    mla_kv_norm = gain(ks[9], (N_MLA_LAYERS, MLA_KV_LORA))
    mla_w_uq = dense(ks[10], (N_MLA_LAYERS, MLA_Q_LORA, MLA_HEADS * (MLA_NOPE_DIM + MLA_ROPE_DIM)), MLA_Q_LORA)
    mla_w_ukv = dense(ks[11], (N_MLA_LAYERS, MLA_KV_LORA, MLA_HEADS * (MLA_NOPE_DIM + MLA_V_DIM)), MLA_KV_LORA)
    mla_w_o = dense(ks[12], (N_MLA_LAYERS, MLA_HEADS * MLA_V_DIM, D_MODEL), MLA_HEADS * MLA_V_DIM)

    ffn_w_in = dense(ks[13], (DEPTH, D_MODEL, 2 * FFN_HIDDEN), D_MODEL)
    ffn_w_out = dense(ks[14], (DEPTH, FFN_HIDDEN, D_MODEL), FFN_HIDDEN)

    return {"x": x, "positions": positions, "norm_gains": norm_gains,
            "diff_w_qkv": diff_w_qkv, "diff_lambda": diff_lambda, "diff_subln": diff_subln, "diff_w_o": diff_w_o,
            "mla_w_down": mla_w_down, "mla_q_norm": mla_q_norm, "mla_kv_norm": mla_kv_norm,
            "mla_w_uq": mla_w_uq, "mla_w_ukv": mla_w_ukv, "mla_w_o": mla_w_o,
            "ffn_w_in": ffn_w_in, "ffn_w_out": ffn_w_out}


def reference(x, positions, norm_gains, diff_w_qkv, diff_lambda, diff_subln, diff_w_o,
              mla_w_down, mla_q_norm, mla_kv_norm, mla_w_uq, mla_w_ukv, mla_w_o,
              ffn_w_in, ffn_w_out):
    for i in range(DEPTH):
        g = norm_gains[i]
        hm = _rmsnorm(x, g[0])
        j = i // N_MIXERS
        if i % N_MIXERS == 0:
            y = _diff_attention(hm, diff_w_qkv[j], diff_lambda[j], diff_subln[j], diff_w_o[j], positions, i)
        else:
            y = _mla(hm, mla_w_down[j], mla_q_norm[j], mla_kv_norm[j], mla_w_uq[j], mla_w_ukv[j], mla_w_o[j], positions)
        x = x + _rmsnorm(y, g[1])
        y = _swiglu(_rmsnorm(x, g[2]), ffn_w_in[i], ffn_w_out[i])
        x = x + _rmsnorm(y, g[3])
    return x
```

```python
import contextlib
import math
import numpy as np
import ml_dtypes
import concourse.bass as bass
import concourse.mybir as mybir
from concourse.bass_utils import run_bass_kernel_spmd

F32 = mybir.dt.float32
BF16 = mybir.dt.bfloat16
I32 = mybir.dt.int32
ALU = mybir.AluOpType
AF = mybir.ActivationFunctionType

NCORE = 8
S = 16384
D = 2048
NT = S // NCORE
NTILE = NT // 128
FFH = 5632
EPS = 1e-6
NEG = -30000.0
ENGS = ("pe", "act", "dve", "pool", "sp")


class Buf:
    __slots__ = ("name", "w", "r", "sem")

    def __init__(self, name=""):
        self.name = name
        self.w = None
        self.r = []
        self.sem = None


class Prog:
    def __init__(self, nc, n_dma_sems=60):
        self.nc = nc
        self.q = {e: [] for e in ENGS}
        self.cnt = {e: 0 for e in ENGS}
        self.gen = {e: 0 for e in ENGS}
        self.seen = {e: {} for e in ENGS}
        self.n_dma_sems = n_dma_sems
        self.streams = {}
        self.dcount = [0] * n_dma_sems
        self.sem_free = list(range(n_dma_sems))
        self.sem_scopes = [[]]
        self.stack = contextlib.ExitStack()
        self.esem = {}
        self.dsem = []
        self.ninst = 0

    def sb(self, name, shape, dt):
        self.uid = getattr(self, "uid", 0) + 1
        return self.stack.enter_context(self.nc.sbuf_tensor("%s_u%d" % (name, self.uid), list(shape), dt))

    def ps(self, name, shape, dt):
        self.uid = getattr(self, "uid", 0) + 1
        return self.stack.enter_context(self.nc.psum_tensor("%s_u%d" % (name, self.uid), list(shape), dt))

    def scope(self):
        return _Scope(self)

    def _deps(self, eng, reads, writes):
        need = {}
        for b in reads:
            if b.w is not None:
                k, v = b.w
                if need.get(k, 0) < v:
                    need[k] = v
        for b in writes:
            if b.w is not None:
                k, v = b.w
                if need.get(k, 0) < v:
                    need[k] = v
            for (k, v) in b.r:
                if need.get(k, 0) < v:
                    need[k] = v
        waits = []
        seen = self.seen[eng]
        for k, v in need.items():
            if k[0] == "e":
                if k[2] < self.gen[k[1]]:
                    continue
                if k[1] == "pe" and eng == "pe":
                    continue
            if seen.get(k, 0) >= v:
                continue
            seen[k] = v
            waits.append((k, v))
        return waits

    def _mark(self, tok, reads, writes):
        for b in reads:
            b.r.append(tok)
            if len(b.r) > 48:
                m = {}
                for k, v in b.r:
                    if m.get(k, 0) < v:
                        m[k] = v
                b.r = list(m.items())
        for b in writes:
            b.w = tok
            b.r = []

    def op(self, eng, fn, reads=(), writes=()):
        waits = self._deps(eng, reads, writes)
        self.cnt[eng] += 1
        key = ("e", eng, self.gen[eng])
        tok = (key, self.cnt[eng])
        self.q[eng].append((waits, fn, key, 1))
        self._mark(tok, reads, writes)
        self.ninst += 1 + len(waits)
        return tok

    def dma(self, eng, fn, reads=(), writes=(), stream=None, inc=16):
        waits = self._deps(eng, reads, writes)
        owner = writes[0] if len(writes) else reads[0]
        if owner.sem is None:
            assert self.sem_free, "out of dma sems"
            owner.sem = self.sem_free.pop(0)
            self.sem_scopes[-1].append((owner.sem, owner))
        idx = owner.sem
        self.dcount[idx] += inc
        key = ("d", idx)
        tok = (key, self.dcount[idx])
        self.q[eng].append((waits, fn, key, inc))
        self._mark(tok, reads, writes)
        self.ninst += 1 + len(waits)
        return tok

    def _sem(self, key):
        if key[0] == "d":
            return self.dsem[key[1]]
        assert key[2] == self.gen[key[1]]
        return self.esem[key[1]]

    def start(self):
        nc = self.nc
        self.semstack = contextlib.ExitStack()
        self.ectx = {}
        for e in ENGS:
            self.ectx[e] = nc.semaphore("s_" + e)
            self.esem[e] = self.ectx[e].__enter__()
        for i in range(self.n_dma_sems):
            self.dsem.append(self.semstack.enter_context(nc.semaphore("d%d" % i)))

    def finish(self):
        self.stack.close()
        for e in ENGS:
            self.ectx[e].__exit__(None, None, None)
        self.semstack.close()

    def flush(self):
        nc = self.nc
        waits = []
        for idx, c in enumerate(self.dcount):
            k = ("d", idx)
            if c > 0 and self.seen["sp"].get(k, 0) < c:
                self.seen["sp"][k] = c
                waits.append((k, c))
        if waits:
            self.q["sp"].append((waits, None, None, 0))
        with nc.Block() as block:
            regs = {"pe": block.tensor, "act": block.scalar, "dve": block.vector,
                    "pool": block.gpsimd, "sp": block.sync}
            for e in ENGS:
                items = self.q[e]
                if not items:
                    continue

                def body(engine, items=items):
                    for waits, fn, skey, inc in items:
                        for k, v in waits:
                            engine.wait_ge(self._sem(k), v)
                        if fn is not None:
                            ins = fn(engine)
                            if inc == 1 and skey[0] == "d":
                                ins.then_inc(self._sem(skey))
                            else:
                                ins.then_inc(self._sem(skey), inc)
                regs[e](body)
        big = [e2 for e2 in ENGS if self.cnt[e2] > 8000]
        if big:
            with nc.Block() as block:
                def clr(engine, big=big):
                    for e2 in big:
                        engine.sem_clear(self.esem[e2])
                block.gpsimd(clr)
            for e2 in big:
                self.gen[e2] += 1
                self.cnt[e2] = 0
        for e in ENGS:
            for e2 in ENGS:
                self.seen[e][("e", e2, self.gen[e2])] = self.cnt[e2]
            for idx, c in enumerate(self.dcount):
                self.seen[e][("d", idx)] = c
        self.q = {e: [] for e in ENGS}


class _Scope:
    def __init__(self, prog):
        self.prog = prog

    def __enter__(self):
        self.saved = self.prog.stack
        self.prog.stack = contextlib.ExitStack()
        self.prog.sem_scopes.append([])
        return self

    def __exit__(self, *a):
        self.prog.flush()
        self.prog.stack.close()
        self.prog.stack = self.saved
        for idx, owner in self.prog.sem_scopes.pop():
            owner.sem = None
            self.prog.sem_free.append(idx)
        return False


def MM(P, out, lhsT, rhs, start, stop, reads, writes):
    P.op("pe", lambda e: e.matmul(out, lhsT=lhsT, rhs=rhs, start=start, stop=stop), reads, writes)


def TR(P, out, in_, ident, reads, writes):
    P.op("pe", lambda e: e.transpose(out=out, in_=in_, identity=ident), reads, writes)


def ACTV(P, out, in_, func, reads, writes, scale=None, bias=None, accum=None):
    kw = {}
    if scale is not None:
        kw["scale"] = scale
    if bias is not None:
        kw["bias"] = bias
    if accum is not None:
        kw["accum_out"] = accum
    P.op("act", lambda e: e.activation(out=out, in_=in_, func=func, **kw), reads, writes)


def DMA(P, q, out, in_, reads, writes, stream):
    P.dma(q, lambda e: e.dma_start(out=out, in_=in_), reads, writes, stream)


def TS(P, eng, out, in0, s1, s2, op0, op1, reads, writes):
    if s2 is None:
        P.op(eng, lambda e: e.tensor_scalar(out=out, in0=in0, scalar1=s1, scalar2=None, op0=op0), reads, writes)
    else:
        P.op(eng, lambda e: e.tensor_scalar(out=out, in0=in0, scalar1=s1, scalar2=s2, op0=op0, op1=op1), reads, writes)


def STT(P, out, in0, scalar, in1, op0, op1, reads, writes):
    P.op("dve", lambda e: e.scalar_tensor_tensor(out=out, in0=in0, scalar=scalar, in1=in1, op0=op0, op1=op1),
         reads, writes)


def TT(P, eng, out, in0, in1, op, reads, writes):
    P.op(eng, lambda e: e.tensor_tensor(out=out, in0=in0, in1=in1, op=op), reads, writes)


def CP(P, eng, out, in_, reads, writes):
    if eng == "act":
        P.op("act", lambda e: e.copy(out=out, in_=in_), reads, writes)
    else:
        P.op(eng, lambda e: e.tensor_copy(out=out, in_=in_), reads, writes)


def MEMSET(P, eng, ap, val, writes):
    P.op(eng, lambda e: e.memset(ap, val), (), writes)


class KB:
    def __init__(self, nc):
        self.nc = nc
        self.P = Prog(nc)
        self.P.start()
        P = self.P
        self.ident = P.sb("ident", [128, 128], BF16)
        self.identb = Buf("ident")
        self.neghalf = P.sb("neghalf", [128, 1], F32)
        self.cb = Buf("consts")
        ident = self.ident
        MEMSET(P, "pool", ident[:], 0.0, [self.identb])
        P.op("pool", lambda e: e.affine_select(out=ident[:], in_=ident[:], pattern=[[-1, 128]],
                                               compare_op=ALU.not_equal, fill=1.0, base=0,
                                               channel_multiplier=1), [self.identb], [self.identb])
        MEMSET(P, "pool", self.neghalf[:], -0.5, [self.cb])
        self.uid = 0

    def name(self, s):
        self.uid += 1
        return "%s_%d" % (s, self.uid)

    def rstd(self, out, outb, ss, ssb, n):
        P = self.P
        TS(P, "pool", out, ss, 1.0 / n, EPS, ALU.mult, ALU.add, [ssb], [outb])
        TT(P, "pool", out, out, self.neghalf[:], ALU.pow, [outb, self.cb], [outb])

    def norm_T(self, src, gsrc, dstT, dstB, ntiles, src_tile0=0, dst_col0=0, Dm=D):
        P = self.P
        KC = Dm // 128
        with P.scope():
            gbc = P.sb(self.name("gbc"), [128, Dm], F32)
            gb = Buf()
            DMA(P, "sp", gbc[:], gsrc.partition_broadcast(128), [], [gb], "nt_g")
            xt = [P.sb(self.name("xt"), [128, Dm], F32) for _ in range(2)]
            xb = [Buf() for _ in range(2)]
            junk = P.sb(self.name("junk"), [128, Dm], BF16)
            jb = Buf()
            hn = [P.sb(self.name("hn"), [128, Dm], BF16) for _ in range(2)]
            hb = [Buf() for _ in range(2)]
            ss = [P.sb(self.name("ss"), [128, 1], F32) for _ in range(2)]
            sb_ = [Buf() for _ in range(2)]
            rs = [P.sb(self.name("rs"), [128, 1], F32) for _ in range(2)]
            rb = [Buf() for _ in range(2)]
            pt = [P.ps(self.name("pt"), [128, 8, 128], BF16) for _ in range(2)]
            pb = [Buf() for _ in range(2)]
            for t in range(ntiles):
                s = t % 2
                r0 = (src_tile0 + t) * 128
                DMA(P, "sp", xt[s][:], src[r0:r0 + 128, :], [], [xb[s]], "nt_x%d" % s)
                ACTV(P, junk[:], xt[s][:], AF.Square, [xb[s]], [jb, sb_[s]], accum=ss[s][:])
                self.rstd(rs[s][:], rb[s], ss[s][:], sb_[s], Dm)
                STT(P, hn[s][:], xt[s][:], rs[s][:, 0:1], gbc[:], ALU.mult, ALU.mult,
                    [xb[s], rb[s], gb], [hb[s]])
                for g in range(KC // 8):
                    k = (t * (KC // 8) + g) % 2
                    for j in range(8):
                        c = g * 8 + j
                        TR(P, pt[k][:, j, :], hn[s][:, c * 128:(c + 1) * 128], self.ident[:],
                           [hb[s], self.identb], [pb[k]])
                    col = dst_col0 + t * 128
                    CP(P, "act" if g % 2 == 0 else "dve", dstT[:, g * 8:(g + 1) * 8, col:col + 128], pt[k][:],
                       [pb[k]], [dstB])

    def load_w(self, dst, dstb, wsrc, stream):
        DMA(self.P, "pool", dst, wsrc.rearrange("(kc p) f -> p kc f", p=128), [], [dstb], stream)

    def proj_F(self, AT, ATb, tok0, ntok, W, blocks, KC, evac, wbufs, psF):
        P = self.P
        ci = 0
        k = 0
        for bi, (c0, n) in enumerate(blocks):
            wt, wb = wbufs[bi % len(wbufs)]
            self.load_w(wt[:, :, 0:n], wb, W[:, c0:c0 + n], "wF%d" % (bi % len(wbufs)))
            for sub in range(n // 128):
                for tb in range(ntok // 512):
                    ps, pb = psF[k % len(psF)]
                    k += 1
                    for kc in range(KC):
                        MM(P, ps[:, 0:512], wt[:, kc, sub * 128:(sub + 1) * 128],
                           AT[:, kc, tok0 + tb * 512: tok0 + (tb + 1) * 512], kc == 0, kc == KC - 1,
                           [wb, ATb], [pb])
                    evac(ci, tb, ps, pb)
                ci += 1

    def proj_T(self, AT, ATb, tok0, ntiles, W, blocks, KC, evac, wbufs, psT):
        P = self.P
        k = 0
        for bi, (c0, n) in enumerate(blocks):
            wt, wb = wbufs[bi % len(wbufs)]
            self.load_w(wt[:, :, 0:n], wb, W[:, c0:c0 + n], "wT%d" % (bi % len(wbufs)))
            for t in range(ntiles):
                ps, pb = psT[k % len(psT)]
                k += 1
                for kc in range(KC):
                    MM(P, ps[:, 0:n], AT[:, kc, tok0 + t * 128: tok0 + (t + 1) * 128], wt[:, kc, 0:n],
                       kc == 0, kc == KC - 1, [wb, ATb], [pb])
                evac(bi, t, ps, pb)

    def attn_layer(self, npass, qparts, kparts, vsrc, dv, scale, slopes, posq, posk, masks_src, fin):
        P = self.P
        nparts = len(qparts(0))
        with P.scope():
            KT = []
            for j in range(nparts):
                rows = kparts(0)[j][2]
                KT.append(P.sb(self.name("KT"), [rows, 32, 512], BF16))
            Cb = [Buf() for _ in range(32)]
            KTb = [Cb for _ in range(nparts)]
            V = P.sb(self.name("V"), [128, 32, 4, dv + 1], BF16)
            Vb = Cb
            for ch in range(32):
                MEMSET(P, "pool", V[:, ch, :, dv:dv + 1], 1.0, [Vb[ch]])
            msk = P.sb(self.name("msk"), [128, 8, 128], F32)
            mb = Buf()
            DMA(P, "sp", msk[:], masks_src, [], [mb], "at_m")
            QT = [[P.sb(self.name("QT"), [qparts(0)[j][1], 512], BF16) for j in range(nparts)] for _ in range(2)]
            QTb = [[b_] * nparts for b_ in (Buf(), Buf())]
            LA = 2
            NB = 4
            psS = [P.ps(self.name("psS"), [128, 512], F32) for _ in range(NB)]
            psSb = [Buf() for _ in range(NB)]
            acc = [P.ps(self.name("acc"), [128, 512], F32) for _ in range(4)]
            accb = [Buf() for _ in range(4)]
            pT = [P.sb(self.name("pT"), [128, 512], BF16) for _ in range(NB)]
            pTb = [Buf() for _ in range(NB)]
            if slopes is not None:
                tt = [P.sb(self.name("tt"), [128, 512], F32) for _ in range(NB)]
                ttb = [Buf() for _ in range(NB)]
                pqi = P.sb(self.name("pqi"), [128, NT], I32)
                pqf = P.sb(self.name("pqf"), [128, NT], F32)
                pqb = Buf()
                DMA(P, "sp", pqi[:], posq.partition_broadcast(128), [], [pqb], "at_p")
                CP(P, "dve", pqf[:], pqi[:], [pqb], [pqb])
                pki = P.sb(self.name("pki"), [128, 128], I32)
                pkf = P.sb(self.name("pkf"), [128, 128], F32)
                pkb = Buf()
                DMA(P, "sp", pki[:], posk, [], [pkb], "at_p2")
                CP(P, "dve", pkf[:], pki[:], [pkb], [pkb])
                kbias = [P.sb(self.name("kbias"), [128, 128], F32) for _ in range(2)]
                kbb = [Buf() for _ in range(2)]
                R = [P.sb(self.name("R"), [128, 512], F32) for _ in range(2)]
                Rb = [Buf() for _ in range(2)]
                Rm = [P.sb(self.name("Rm"), [128, 32, 128], F32) for _ in range(2)]
                Rmb = [Buf() for _ in range(2)]
            it = 0

            def load_chunk(p, r, lc):
                kp = kparts(p)
                vs, vc0 = vsrc(p)
                ch = r * 4 + lc
                for j in range(nparts):
                    src, rb_, rows, rpr = kp[j]
                    DMA(P, "sp", KT[j][:, ch, :],
                        src[r * rpr + rb_: r * rpr + rb_ + rows, lc * 512:(lc + 1) * 512],
                        [], [KTb[j][ch]], "")
                DMA(P, "sp", V[:, ch, :, 0:dv],
                    vs[r * NT + lc * 512: r * NT + (lc + 1) * 512, vc0:vc0 + dv].rearrange(
                        "(t p) d -> p t d", p=128),
                    [], [Vb[ch]], "")

            def prep_group(gi):
                p, m = gi // 4, gi % 4
                qq = gi % 2
                qp = qparts(p)
                for j in range(nparts):
                    DMA(P, "sp", QT[qq][j][:], qp[j][0][:, m * 512:(m + 1) * 512], [], [QTb[qq][j]], "")
                if slopes is not None:
                    sl = slopes[p]
                    kk = p % 2
                    if m == 0:
                        TS(P, "pool", kbias[kk][:], pkf[:], sl / scale, None, ALU.mult, None, [pkb], [kbb[kk]])
                    TS(P, "pool", R[qq][:], pqf[:, m * 512:(m + 1) * 512], -sl / scale, None, ALU.mult, None,
                       [pqb], [Rb[qq]])
                    for ip in range(4):
                        for r in range(8):
                            TT(P, "pool", Rm[qq][:, ip * 8 + r, :], R[qq][:, ip * 128:(ip + 1) * 128],
                               msk[:, r, :], ALU.add, [Rb[qq], mb], [Rmb[qq]])

            for lc in range(4):
                for r in range(8):
                    load_chunk(0, r, lc)
            prep_group(0)
            ngroups = npass * 4
            for gi in range(ngroups):
                p, m = gi // 4, gi % 4
                qq = gi % 2
                kk = p % 2
                if gi + 1 < ngroups:
                    prep_group(gi + 1)
                if True:
                    tiles = []
                    reload_after = {}
                    for r in range(8):
                        for lt in range(4 * m):
                            tiles.append((r * 4 + lt // 4, lt % 4, -1, r))
                        if m == 3 and p + 1 < npass:
                            reload_after[len(tiles) - 1] = [(r, lc) for lc in range(3)]
                    for ip in range(4):
                        for r in range(8):
                            tiles.append((r * 4 + m, ip, ip, r))
                    if m == 3 and p + 1 < npass:
                        reload_after[len(tiles) - 1] = [(r, 3) for r in range(8)]
                    ntl = len(tiles)
                    pendq = []
                    for ti in range(ntl + LA):
                        if ti < ntl:
                            ch, tl, ip, r = tiles[ti]
                            q0 = 0 if ip < 0 else ip * 128
                            n = 512 - q0
                            sb_i = it % NB
                            it += 1
                            ps, pb = psS[sb_i], psSb[sb_i]
                            for j in range(nparts):
                                MM(P, ps[:, 0:n], KT[j][:, ch, tl * 128:(tl + 1) * 128], QT[qq][j][:, q0:512],
                                   j == 0, j == nparts - 1, [KTb[j][ch], QTb[qq][j]], [pb])
                            if slopes is not None:
                                ktcol = r * 16 + (ch % 4) * 4 + tl
                                if ip < 0:
                                    STT(P, tt[sb_i][:, 0:n], ps[:, 0:n], kbias[kk][:, ktcol:ktcol + 1],
                                        R[qq][:, q0:512], ALU.add, ALU.add, [pb, kbb[kk], Rb[qq]], [ttb[sb_i]])
                                else:
                                    STT(P, tt[sb_i][:, 0:128], ps[:, 0:128], kbias[kk][:, ktcol:ktcol + 1],
                                        Rm[qq][:, ip * 8 + r, :], ALU.add, ALU.add, [pb, kbb[kk], Rmb[qq]],
                                        [ttb[sb_i]])
                                    if n > 128:
                                        STT(P, tt[sb_i][:, 128:n], ps[:, 128:n], kbias[kk][:, ktcol:ktcol + 1],
                                            R[qq][:, q0 + 128:512], ALU.add, ALU.add, [pb, kbb[kk], Rb[qq]],
                                            [ttb[sb_i]])
                                ACTV(P, pT[sb_i][:, 0:n], tt[sb_i][:, 0:n], AF.Exp, [ttb[sb_i]], [pTb[sb_i]],
                                     scale=scale)
                            else:
                                if ip >= 0:
                                    TT(P, "dve", ps[:, 0:128], ps[:, 0:128], msk[:, r, :], ALU.add, [pb, mb], [pb])
                                ACTV(P, pT[sb_i][:, 0:n], ps[:, 0:n], AF.Exp, [pb], [pTb[sb_i]], scale=scale)
                            pendq.append((sb_i, ch, tl, ip, r, ti))
                        if ti >= LA:
                            sbp, chp, tlp, ipp, rp, tip = pendq.pop(0)
                            i0 = 0 if ipp < 0 else ipp
                            for i in range(i0, 4):
                                first = (tip == 0)
                                last = (ipp == i and rp == 7)
                                MM(P, acc[i][:, 0:dv + 1], pT[sbp][:, (i - i0) * 128:(i - i0 + 1) * 128],
                                   V[:, chp, tlp, 0:dv + 1], first, last, [pTb[sbp], Vb[chp]], [accb[i]])
                                if last:
                                    fin(p, m, i, acc[i], accb[i])
                            for (r_, lc_) in reload_after.get(tip, ()):
                                load_chunk(p + 1, r_, lc_)
                if m == 3 and (p + 1) % 4 == 0 and p + 1 < npass:
                    P.flush()


def dram_in(nc, name, shape, dt):
    return nc.dram_tensor(name, list(shape), dt, kind="ExternalInput").ap()


def dram_out(nc, name, shape, dt):
    return nc.dram_tensor(name, list(shape), dt, kind="ExternalOutput").ap()


def qkv_proj(K, x, g0, w, qt, kt, v):
    P = K.P
    with P.scope():
        hT = P.sb("hT", [128, 16, NT], BF16)
        hTb = Buf()
        K.norm_T(x, g0, hT, hTb, NTILE)
        with P.scope():
            wbufs = [(P.sb("wb%d" % i, [128, 16, 512], BF16), Buf()) for i in range(2)]
            psF = [(P.ps("psF%d" % i, [128, 512], F32), Buf()) for i in range(4)]
            stg = [(P.sb("stg%d" % i, [128, 512], BF16), Buf()) for i in range(4)]
            cnt = [0]

            def evac_qk(ci, tb, ps, pb):
                k = cnt[0] % 4
                cnt[0] += 1
                st, sb_ = stg[k]
                CP(P, "act" if k % 2 == 0 else "dve", st[:], ps[:, 0:512], [pb], [sb_])
                dst = qt if ci < 16 else kt
                r0 = (ci % 16) * 128
                DMA(P, "sp", dst[r0:r0 + 128, tb * 512:(tb + 1) * 512], st[:], [sb_], [], "")

            K.proj_F(hT, hTb, 0, NT, w, [(c * 512, 512) for c in range(8)], 16, evac_qk, wbufs, psF)

            def evac_v(bi, t, ps, pb):
                k = cnt[0] % 4
                cnt[0] += 1
                st, sb_ = stg[k]
                CP(P, "act" if k % 2 == 0 else "dve", st[:], ps[:, 0:512], [pb], [sb_])
                DMA(P, "sp", v[t * 128:(t + 1) * 128, bi * 512:(bi + 1) * 512], st[:], [sb_], [], "")

            K.proj_T(hT, hTb, 0, NTILE, w, [(4096 + c * 512, 512) for c in range(4)], 16, evac_v, wbufs, psF)


def dram_tmp(nc, name, shape, dt):
    return nc.dram_tensor(name, list(shape), dt, kind="Internal").ap()


DBG = None
LAMBDA_INIT0 = 0.8 - 0.6 * math.exp(-0.3 * 0)


def diff_attention(K, qt, ktg, vg, posq, posk, masks, lam, subln, o_d):
    P = K.P
    scale = 128 ** -0.5
    with P.scope():
        lamt = P.sb("lamt", [128, 4, 128], F32)
        lb = Buf()
        DMA(P, "sp", lamt[:], lam.partition_broadcast(128), [], [lb], "da_c")
        lj = P.sb("lj", [128, 128], F32)
        s12 = P.sb("s12", [128, 2], F32)
        e12 = P.sb("e12", [128, 2], F32)
        neglam = P.sb("neglam", [128, 1], F32)
        nlb = Buf()
        for i in range(2):
            P.op("dve", lambda e, i=i: e.scalar_tensor_tensor(out=lj[:], in0=lamt[:, 2 * i, :], scalar=1.0,
                                                              in1=lamt[:, 2 * i + 1, :], op0=ALU.mult, op1=ALU.mult,
                                                              accum_out=s12[:, i:i + 1]), [lb], [lb])
        ACTV(P, e12[:], s12[:], AF.Exp, [lb], [lb])
        TT(P, "dve", neglam[:], e12[:, 1:2], e12[:, 0:1], ALU.subtract, [lb], [nlb])
        TS(P, "dve", neglam[:], neglam[:], -LAMBDA_INIT0, None, ALU.add, None, [nlb], [nlb])
        gsub = P.sb("gsub", [128, 256], F32)
        gsb = Buf()
        DMA(P, "sp", gsub[:], subln.partition_broadcast(128), [], [gsb], "da_c")
        TS(P, "dve", gsub[:], gsub[:], 1.0 - LAMBDA_INIT0, None, ALU.mult, None, [gsb], [gsb])
        O1 = P.sb("O1", [128, NTILE, 256], F32)
        O1b = [Buf() for _ in range(NTILE)]
        rl = [P.sb("rl%d" % i, [128, 1], F32) for i in range(2)]
        rlb = [Buf() for _ in range(2)]
        dt_ = [P.sb("dt%d" % i, [128, 256], F32) for i in range(2)]
        dtb = [Buf() for _ in range(2)]
        dj = P.sb("dj", [128, 256], F32)
        djb = Buf()
        ssd = [P.sb("ssd%d" % i, [128, 1], F32) for i in range(2)]
        ssb = [Buf() for _ in range(2)]
        rsd = [P.sb("rsd%d" % i, [128, 1], F32) for i in range(2)]
        rsb = [Buf() for _ in range(2)]
        ost = [P.sb("ost%d" % i, [128, 256], BF16) for i in range(4)]
        osb = [Buf() for _ in range(4)]
        cnt = [0]

        def fin(p, m, i, acc, accb):
            h, s = p // 2, p % 2
            jt = 4 * m + i
            k = cnt[0] % 2
            k4 = cnt[0] % 4
            cnt[0] += 1
            if DBG is not None and p == DBG[1]:
                dbt = P.sb(K.name("dbt"), [128, 257], F32)
                dbb = Buf()
                CP(P, "dve", dbt[:], acc[:, 0:257], [accb], [dbb])
                DMA(P, "sp", DBG[0][jt * 128:(jt + 1) * 128, :], dbt[:], [dbb], [], "dbg")
            P.op("dve", lambda e: e.reciprocal(out=rl[k][:], in_=acc[:, 256:257]), [accb], [rlb[k]])
            if s == 0:
                TS(P, "dve", O1[:, jt, :], acc[:, 0:256], rl[k][:, 0:1], None, ALU.mult, None,
                   [accb, rlb[k]], [O1b[jt]])
            else:
                TT(P, "dve", rl[k][:], rl[k][:], neglam[:], ALU.mult, [rlb[k], nlb], [rlb[k]])
                STT(P, dt_[k][:], acc[:, 0:256], rl[k][:, 0:1], O1[:, jt, :], ALU.mult, ALU.add,
                    [accb, rlb[k], O1b[jt]], [dtb[k]])
                P.op("dve", lambda e: e.scalar_tensor_tensor(out=dj[:], in0=dt_[k][:], scalar=1.0, in1=dt_[k][:],
                                                             op0=ALU.mult, op1=ALU.mult, accum_out=ssd[k][:]),
                     [dtb[k]], [djb, ssb[k]])
                K.rstd(rsd[k][:], rsb[k], ssd[k][:], ssb[k], 256)
                STT(P, ost[k4][:], dt_[k][:], rsd[k][:, 0:1], gsub[:], ALU.mult, ALU.mult,
                    [dtb[k], rsb[k], gsb], [osb[k4]])
                DMA(P, "sp", o_d[jt * 128:(jt + 1) * 128, h * 256:(h + 1) * 256], ost[k4][:], [osb[k4]], [],
                    "da_o%d" % k4)

        slopes = [2.0 ** (-(p // 2 + 1)) for p in range(16)]
        K.attn_layer(16,
                     lambda p: [(qt[p * 128:(p + 1) * 128, :], 128)],
                     lambda p: [(ktg, p * 128, 128, 2048)],
                     lambda p: (vg, (p // 2) * 256),
                     256, scale, slopes, posq, posk, masks, fin)


def loadT(K, src, tile0, ntiles, dstT, dstB, Dm=D):
    P = K.P
    KC = Dm // 128
    with P.scope():
        ot = [P.sb(K.name("lt_o"), [128, Dm], BF16) for _ in range(2)]
        ob = [Buf() for _ in range(2)]
        pt = [P.ps(K.name("lt_p"), [128, 8, 128], BF16) for _ in range(2)]
        pb = [Buf() for _ in range(2)]
        for t in range(ntiles):
            s = t % 2
            r0 = (tile0 + t) * 128
            DMA(P, "sp", ot[s][:], src[r0:r0 + 128, :], [], [ob[s]], "")
            for g in range(KC // 8):
                k = (t * (KC // 8) + g) % 2
                for j in range(8):
                    c = g * 8 + j
                    TR(P, pt[k][:, j, :], ot[s][:, c * 128:(c + 1) * 128], K.ident[:], [ob[s], K.identb], [pb[k]])
                CP(P, "act" if g % 2 == 0 else "dve", dstT[:, g * 8:(g + 1) * 8, t * 128:(t + 1) * 128], pt[k][:],
                   [pb[k]], [dstB])


def attn_out(K, o_d, w_o, x_in, x1_d, h2_d, g1, g2):
    P = K.P
    with P.scope():
        Wo = P.sb("Wo", [128, 16, 2048], BF16)
        Wob = [Buf() for _ in range(4)]
        for c in range(4):
            K.load_w(Wo[:, :, c * 512:(c + 1) * 512], Wob[c], w_o[:, c * 512:(c + 1) * 512], "")
        gbc1 = P.sb("gbc1", [128, D], F32)
        gbc2 = P.sb("gbc2", [128, D], F32)
        g1b, g2b = Buf(), Buf()
        DMA(P, "sp", gbc1[:], g1.partition_broadcast(128), [], [g1b], "")
        DMA(P, "sp", gbc2[:], g2.partition_broadcast(128), [], [g2b], "")
        ot = [P.sb("ao_o%d" % i, [128, D], BF16) for i in range(2)]
        ob = [Buf() for _ in range(2)]
        OT = [P.sb("ao_T%d" % i, [128, 16, 128], BF16) for i in range(2)]
        OTb = [Buf() for _ in range(2)]
        xt = [P.sb("ao_x%d" % i, [128, D], F32) for i in range(2)]
        xb = [Buf() for _ in range(2)]
        y1 = P.sb("ao_y", [128, D], F32)
        yb = Buf()
        hn = [P.sb("ao_h%d" % i, [128, D], BF16) for i in range(2)]
        hb = [Buf() for _ in range(2)]
        junk = P.sb("ao_j", [128, D], BF16)
        jb = Buf()
        ss = [P.sb("ao_ss%d" % i, [128, 4], F32) for i in range(2)]
        ssb = [Buf() for _ in range(2)]
        rs = [P.sb("ao_rs%d" % i, [128, 2], F32) for i in range(2)]
        rsb = [Buf() for _ in range(2)]
        psY = [P.ps("ao_py%d" % i, [128, 512], F32) for i in range(4)]
        pyb = [Buf() for _ in range(4)]
        pt = [P.ps("ao_pt%d" % i, [128, 8, 128], BF16) for i in range(2)]
        ptb = [Buf() for _ in range(2)]
        for t in range(NTILE):
            s = t % 2
            r0 = t * 128
            DMA(P, "sp", ot[s][:], o_d[r0:r0 + 128, :], [], [ob[s]], "")
            DMA(P, "sp", xt[s][:], x_in[r0:r0 + 128, :], [], [xb[s]], "")
            for g in range(2):
                k = g
                for j in range(8):
                    c = g * 8 + j
                    TR(P, pt[k][:, j, :], ot[s][:, c * 128:(c + 1) * 128], K.ident[:], [ob[s], K.identb], [ptb[k]])
                CP(P, "act" if g == 0 else "dve", OT[s][:, g * 8:(g + 1) * 8, :], pt[k][:], [ptb[k]], [OTb[s]])
            for cb in range(4):
                for kc in range(16):
                    MM(P, psY[cb][:, :], OT[s][:, kc, :], Wo[:, kc, cb * 512:(cb + 1) * 512], kc == 0, kc == 15,
                       [OTb[s], Wob[cb]], [pyb[cb]])
                ACTV(P, junk[:, cb * 512:(cb + 1) * 512], psY[cb][:, :], AF.Square, [pyb[cb]], [jb, ssb[s]],
                     accum=ss[s][:, cb:cb + 1])
            TT(P, "pool", ss[s][:, 0:2], ss[s][:, 0:2], ss[s][:, 2:4], ALU.add, [ssb[s]], [ssb[s]])
            TT(P, "pool", ss[s][:, 0:1], ss[s][:, 0:1], ss[s][:, 1:2], ALU.add, [ssb[s]], [ssb[s]])
            K.rstd(rs[s][:, 0:1], rsb[s], ss[s][:, 0:1], ssb[s], D)
            for cb in range(4):
                STT(P, y1[:, cb * 512:(cb + 1) * 512], psY[cb][:, :], rs[s][:, 0:1], gbc1[:, cb * 512:(cb + 1) * 512],
                    ALU.mult, ALU.mult, [pyb[cb], rsb[s], g1b], [yb])
            TT(P, "pool", xt[s][:], xt[s][:], y1[:], ALU.add, [xb[s], yb], [xb[s]])
            DMA(P, "sp", x1_d[r0:r0 + 128, :], xt[s][:], [xb[s]], [], "")
            ACTV(P, junk[:], xt[s][:], AF.Square, [xb[s]], [jb, ssb[s]], accum=ss[s][:, 3:4])
            K.rstd(rs[s][:, 1:2], rsb[s], ss[s][:, 3:4], ssb[s], D)
            STT(P, hn[s][:], xt[s][:], rs[s][:, 1:2], gbc2[:], ALU.mult, ALU.mult, [xb[s], rsb[s], g2b], [hb[s]])
            DMA(P, "sp", h2_d[r0:r0 + 128, :], hn[s][:], [hb[s]], [], "")


def ffn(K, h2_d, x1_d, x_out, g3, w_in, w_out):
    P = K.P
    for half in range(2):
        with P.scope():
            actT = P.sb("actT", [128, 44, 1024], BF16)
            actb = Buf()
            with P.scope():
                h2T = P.sb("h2T", [128, 16, 1024], BF16)
                h2Tb = Buf()
                loadT(K, h2_d, 8 * half, 8, h2T, h2Tb)
                wg = [P.sb("wg%d" % i, [128, 16, 256], BF16) for i in range(2)]
                wu = [P.sb("wu%d" % i, [128, 16, 256], BF16) for i in range(2)]
                wgb = [Buf() for _ in range(2)]
                wub = [Buf() for _ in range(2)]
                sg = [P.sb("sg%d" % i, [128, 512], F32) for i in range(2)]
                sgb = [Buf() for _ in range(2)]
                psG = [P.ps("psG%d" % i, [128, 512], F32) for i in range(2)]
                psU = [P.ps("psU%d" % i, [128, 512], F32) for i in range(2)]
                pgb = [Buf() for _ in range(2)]
                pub = [Buf() for _ in range(2)]
                cnt = 0
                for jb in range(22):
                    sl = jb % 2
                    K.load_w(wg[sl][:], wgb[sl], w_in[:, jb * 256:(jb + 1) * 256], "")
                    K.load_w(wu[sl][:], wub[sl], w_in[:, FFH + jb * 256:FFH + (jb + 1) * 256], "")
                    for sub in range(2):
                        j = 2 * jb + sub
                        for tb in range(2):
                            k = cnt % 2
                            cnt += 1
                            for kc in range(16):
                                MM(P, psG[k][:, :], wg[sl][:, kc, sub * 128:(sub + 1) * 128],
                                   h2T[:, kc, tb * 512:(tb + 1) * 512], kc == 0, kc == 15, [wgb[sl], h2Tb], [pgb[k]])
                            for kc in range(16):
                                MM(P, psU[k][:, :], wu[sl][:, kc, sub * 128:(sub + 1) * 128],
                                   h2T[:, kc, tb * 512:(tb + 1) * 512], kc == 0, kc == 15, [wub[sl], h2Tb], [pub[k]])
                            ACTV(P, sg[k][:], psG[k][:, :], AF.Silu, [pgb[k]], [sgb[k]])
                            TT(P, "dve", actT[:, j, tb * 512:(tb + 1) * 512], sg[k][:], psU[k][:, :], ALU.mult,
                               [sgb[k], pub[k]], [actb])
            with P.scope():
                wo = [P.sb("wo%d" % i, [128, 44, 256], BF16) for i in range(2)]
                wob = [Buf() for _ in range(2)]
                ysb = P.sb("ysb", [128, 4, D], F32)
                yb = [Buf() for _ in range(4)]
                xt = [P.sb("ff_x%d" % i, [128, D], F32) for i in range(2)]
                xb = [Buf() for _ in range(2)]
                gbc3 = P.sb("gbc3", [128, D], F32)
                g3b = Buf()
                DMA(P, "sp", gbc3[:], g3.partition_broadcast(128), [], [g3b], "")
                junk = P.sb("ff_j", [128, D], BF16)
                jb_ = Buf()
                ss = [P.sb("ff_ss%d" % i, [128, 1], F32) for i in range(2)]
                ssb = [Buf() for _ in range(2)]
                rs = [P.sb("ff_rs%d" % i, [128, 1], F32) for i in range(2)]
                rsb = [Buf() for _ in range(2)]
                psO = [P.ps("psO%d" % i, [128, 512], F32) for i in range(2)]
                pob = [Buf() for _ in range(2)]
                cnt = 0
                wl = 0
                for quarter in range(2):
                    for cb in range(8):
                        sl = wl % 2
                        wl += 1
                        DMA(P, "pool", wo[sl][:], w_out[:, cb * 256:(cb + 1) * 256].rearrange("(j p) c -> p j c", p=128),
                            [], [wob[sl]], "")
                        for t in range(4):
                            c0 = quarter * 512 + t * 128
                            k = cnt % 2
                            cnt += 1
                            for j in range(44):
                                MM(P, psO[k][:, 0:256], actT[:, j, c0:c0 + 128], wo[sl][:, j, :], j == 0, j == 43,
                                   [actb, wob[sl]], [pob[k]])
                            CP(P, "act" if k == 0 else "dve", ysb[:, t, cb * 256:(cb + 1) * 256], psO[k][:, 0:256],
                               [pob[k]], [yb[t]])
                    for t in range(4):
                        s = t % 2
                        r0 = (half * 8 + quarter * 4 + t) * 128
                        DMA(P, "sp", xt[s][:], x1_d[r0:r0 + 128, :], [], [xb[s]], "")
                        ACTV(P, junk[:], ysb[:, t, :], AF.Square, [yb[t]], [jb_, ssb[s]], accum=ss[s][:])
                        K.rstd(rs[s][:], rsb[s], ss[s][:], ssb[s], D)
                        STT(P, ysb[:, t, :], ysb[:, t, :], rs[s][:, 0:1], gbc3[:], ALU.mult, ALU.mult,
                            [yb[t], rsb[s], g3b], [yb[t]])
                        TT(P, "pool", xt[s][:], xt[s][:], ysb[:, t, :], ALU.add, [xb[s], yb[t]], [xb[s]])
                        DMA(P, "sp", x_out[r0:r0 + 128, :], xt[s][:], [xb[s]], [], "")


TWO_PI = 2.0 * math.pi


def mla_pre(K, x_d, g0, w_down, qkn, w_uq, w_ukv, posq, ropec, qn_d, qr_d, kn_d, kr_d, v_d):
    P = K.P
    with P.scope():
        hT = P.sb("m_hT", [128, 16, NT], BF16)
        hTb = Buf()
        K.norm_T(x_d, g0, hT, hTb, NTILE)
        cn = P.sb("m_cn", [128, 8, NT], BF16)
        cnb = Buf()
        cosT = P.sb("m_cos", [64, NT], F32)
        sinT = P.sb("m_sin", [64, NT], F32)
        tabb = Buf()
        with P.scope():
            pqi = P.sb("m_pqi", [64, NT], I32)
            pqf = P.sb("m_pqf", [64, NT], F32)
            pb_ = Buf()
            DMA(P, "sp", pqi[:], posq.partition_broadcast(64), [], [pb_], "")
            CP(P, "dve", pqf[:], pqi[:], [pb_], [pb_])
            rc = P.sb("m_rc", [64, 3], F32)
            rcb = Buf()
            DMA(P, "sp", rc[:], ropec, [], [rcb], "")
            v = P.sb("m_v", [64, NT], F32)
            vi = P.sb("m_vi", [64, NT], I32)
            vf = P.sb("m_vf", [64, NT], F32)
            vb = Buf()
            for dst, col in ((cosT, 1), (sinT, 2)):
                TS(P, "dve", v[:], pqf[:], rc[:, 0:1], rc[:, col:col + 1], ALU.mult, ALU.add, [pb_, rcb], [vb])
                CP(P, "dve", vi[:], v[:], [vb], [vb])
                CP(P, "dve", vf[:], vi[:], [vb], [vb])
                TT(P, "dve", v[:], v[:], vf[:], ALU.subtract, [vb], [vb])
                STT(P, vf[:], v[:], 0.5, v[:], ALU.is_gt, ALU.subtract, [vb], [vb])
                ACTV(P, dst[:], vf[:], AF.Sin, [vb], [tabb], scale=-TWO_PI)
        stg = [(P.sb("m_stg%d" % i, [128, 512], BF16), Buf()) for i in range(4)]
        rt = [(P.sb("m_rt%d" % i, [64, 2, 512], F32), Buf()) for i in range(2)]
        cnt = [0]

        def evac_to(dst_d, r0, rows, tb, ps, pb):
            k = cnt[0] % 4
            cnt[0] += 1
            st, sb_ = stg[k]
            CP(P, "act" if k % 2 == 0 else "dve", st[0:rows, :], ps[0:rows, 0:512], [pb], [sb_])
            DMA(P, "sp", dst_d[r0:r0 + rows, tb * 512:(tb + 1) * 512], st[0:rows, :], [sb_], [], "")

        def rope_out(dst_d, r0, tb, psA, pab, psB, pbb):
            k = cnt[0] % 4
            cnt[0] += 1
            st, sb_ = stg[k]
            r_, rb_ = rt[k % 2]
            TT(P, "dve", r_[:, 0, :], psA[0:64, 0:512], cosT[:, tb * 512:(tb + 1) * 512], ALU.mult, [pab, tabb], [rb_])
            TT(P, "dve", r_[:, 1, :], psB[0:64, 0:512], sinT[:, tb * 512:(tb + 1) * 512], ALU.mult, [pbb, tabb], [rb_])
            TT(P, "pool", st[0:64, :], r_[:, 0, :], r_[:, 1, :], ALU.add, [rb_], [sb_])
            DMA(P, "sp", dst_d[r0:r0 + 64, tb * 512:(tb + 1) * 512], st[0:64, :], [sb_], [], "")

        with P.scope():
            wbufs = [(P.sb("m_wb%d" % i, [128, 16, 512], BF16), Buf()) for i in range(2)]
            psC = [(P.ps("m_psC%d" % i, [128, 512], F32), Buf()) for i in range(4)]
            psN = [(P.ps("m_psN%d" % i, [128, 512], F32), Buf()) for i in range(2)]
            psR = [(P.ps("m_psR%d" % i, [128, 512], F32), Buf()) for i in range(2)]
            wr = P.sb("m_wr", [128, 16, 64], BF16)
            ws = P.sb("m_ws", [128, 16, 64], BF16)
            wrb = Buf()
            wsb = Buf()
            K.load_w(wr[:], wrb, w_down[:, 1024:1088], "")
            K.load_w(ws[:, :, 0:32], wsb, w_down[:, 1056:1088], "")
            K.load_w(ws[:, :, 32:64], wsb, w_down[:, 1024:1056], "")
            for tb in range(4):
                (psA, pab), (psB, pbb) = psR[0], psR[1]
                for kc in range(16):
                    MM(P, psA[0:64, 0:512], wr[:, kc, :], hT[:, kc, tb * 512:(tb + 1) * 512], kc == 0, kc == 15,
                       [wrb, hTb], [pab])
                for kc in range(16):
                    MM(P, psB[0:64, 0:512], ws[:, kc, :], hT[:, kc, tb * 512:(tb + 1) * 512], kc == 0, kc == 15,
                       [wsb, hTb], [pbb])
                rope_out(kr_d, 0, tb, psA, pab, psB, pbb)
            ones = P.sb("m_ones", [128, 128], BF16)
            onb = Buf()
            MEMSET(P, "pool", ones[:], 1.0, [onb])
            nh = P.sb("m_nh", [128, 512], F32)
            nhb = Buf()
            MEMSET(P, "pool", nh[:], -0.5, [nhb])
            gn = P.sb("m_gn", [128, 8], F32)
            gnb = Buf()
            DMA(P, "sp", gn[:], qkn, [], [gnb], "")
            sq = [(P.sb("m_sq%d" % i, [128, 512], BF16), Buf()) for i in range(4)]
            vv = [(P.sb("m_vv%d" % i, [128, 512], F32), Buf()) for i in range(2)]
            it = 0
            for half in range(2):
                wt, wb = wbufs[half]
                K.load_w(wt[:], wb, w_down[:, half * 512:(half + 1) * 512], "")
                for tb in range(4):
                    pn, pnb = psN[it % 2]
                    vt, vtb = vv[it % 2]
                    it += 1
                    for ci in range(4):
                        ps, pb = psC[ci]
                        for kc in range(16):
                            MM(P, ps[:, 0:512], wt[:, kc, ci * 128:(ci + 1) * 128], hT[:, kc, tb * 512:(tb + 1) * 512],
                               kc == 0, kc == 15, [wb, hTb], [pb])
                        ACTV(P, sq[ci][0][:], ps[:, 0:512], AF.Square, [pb], [sq[ci][1]])
                    for ci in range(4):
                        MM(P, pn[:, 0:512], ones[:], sq[ci][0][:], ci == 0, ci == 3, [onb, sq[ci][1]], [pnb])
                    TS(P, "dve", vt[:], pn[:, 0:512], 1.0 / 512, EPS, ALU.mult, ALU.add, [pnb], [vtb])
                    TT(P, "pool", vt[:], vt[:], nh[:], ALU.pow, [vtb, nhb], [vtb])
                    for ci in range(4):
                        c = half * 4 + ci
                        ps, pb = psC[ci]
                        STT(P, cn[:, c, tb * 512:(tb + 1) * 512], ps[:, 0:512], gn[:, c:c + 1],
                            vt[:], ALU.mult, ALU.mult, [pb, gnb, vtb], [cnb])
        with P.scope():
            wq = P.sb("m_wq", [128, 4, 3072], BF16)
            wqb = Buf()
            for c in range(6):
                K.load_w(wq[:, :, c * 512:(c + 1) * 512], wqb, w_uq[:, c * 512:(c + 1) * 512], "")
            wqs = P.sb("m_wqs", [128, 4, 16, 64], BF16)
            wqsb = Buf()
            uq3 = w_uq.rearrange("(kc p) (h d) -> kc p h d", p=128, d=192)
            for kc in range(4):
                DMA(P, "pool", wqs[:, kc, :, 0:32], uq3[kc, :, :, 160:192], [], [wqsb], "")
                DMA(P, "pool", wqs[:, kc, :, 32:64], uq3[kc, :, :, 128:160], [], [wqsb], "")
            psQ = [(P.ps("m_psQ%d" % i, [128, 512], F32), Buf()) for i in range(6)]
            it = 0
            for h in range(16):
                for tb in range(4):
                    (ps, pb), (psA, pab), (psB, pbb) = psQ[3 * (it % 2)], psQ[3 * (it % 2) + 1], psQ[3 * (it % 2) + 2]
                    it += 1
                    for kc in range(4):
                        MM(P, ps[:, 0:512], wq[:, kc, h * 192:h * 192 + 128], cn[:, kc, tb * 512:(tb + 1) * 512],
                           kc == 0, kc == 3, [wqb, cnb], [pb])
                    for kc in range(4):
                        MM(P, psA[0:64, 0:512], wq[:, kc, h * 192 + 128:h * 192 + 192],
                           cn[:, kc, tb * 512:(tb + 1) * 512], kc == 0, kc == 3, [wqb, cnb], [pab])
                    for kc in range(4):
                        MM(P, psB[0:64, 0:512], wqs[:, kc, h, :], cn[:, kc, tb * 512:(tb + 1) * 512],
                           kc == 0, kc == 3, [wqsb, cnb], [pbb])
                    evac_to(qn_d, h * 128, 128, tb, ps, pb)
                    rope_out(qr_d, h * 64, tb, psA, pab, psB, pbb)
        with P.scope():
            wkn = P.sb("m_wkn", [128, 4, 16, 128], BF16)
            wv = P.sb("m_wv", [128, 4, 16, 128], BF16)
            wkb = Buf()
            wvb = Buf()
            kv3 = w_ukv.rearrange("(kc p) (h d) -> kc p h d", p=128, d=256)
            for kc in range(4):
                DMA(P, "pool", wkn[:, kc, :, :], kv3[kc, :, :, 0:128], [], [wkb], "")
                DMA(P, "pool", wv[:, kc, :, :], kv3[kc, :, :, 128:256], [], [wvb], "")
            psK = [(P.ps("m_psK%d" % i, [128, 512], F32), Buf()) for i in range(4)]
            it = 0
            for h in range(16):
                for tb in range(4):
                    ps, pb = psK[it % 4]
                    it += 1
                    for kc in range(4):
                        MM(P, ps[:, 0:512], wkn[:, kc, h, :], cn[:, 4 + kc, tb * 512:(tb + 1) * 512],
                           kc == 0, kc == 3, [wkb, cnb], [pb])
                    evac_to(kn_d, h * 128, 128, tb, ps, pb)
            for t in range(NTILE):
                for hb in range(4):
                    ps, pb = psK[it % 4]
                    it += 1
                    for kc in range(4):
                        MM(P, ps[:, 0:512], cn[:, 4 + kc, t * 128:(t + 1) * 128],
                           wv[:, kc, 4 * hb:4 * hb + 4, :].rearrange("p h d -> p (h d)"),
                           kc == 0, kc == 3, [wvb, cnb], [pb])
                    k = cnt[0] % 4
                    cnt[0] += 1
                    st, sb_ = stg[k]
                    CP(P, "act" if k % 2 == 0 else "dve", st[:], ps[:, 0:512], [pb], [sb_])
                    DMA(P, "sp", v_d[t * 128:(t + 1) * 128, hb * 512:(hb + 1) * 512], st[:], [sb_], [], "")


def mla_attention(K, qn, qr, kng, krg, vg, masks, o_d):
    P = K.P
    scale = 192 ** -0.5
    with P.scope():
        rl = [P.sb("ma_rl%d" % i, [128, 1], F32) for i in range(2)]
        rlb = [Buf() for _ in range(2)]
        ost = [P.sb("ma_o%d" % i, [128, 128], BF16) for i in range(4)]
        osb = [Buf() for _ in range(4)]
        cnt = [0]

        def fin(p, m, i, acc, accb):
            jt = 4 * m + i
            k = cnt[0] % 2
            k4 = cnt[0] % 4
            cnt[0] += 1
            P.op("dve", lambda e: e.reciprocal(out=rl[k][:], in_=acc[:, 128:129]), [accb], [rlb[k]])
            TS(P, "dve", ost[k4][:], acc[:, 0:128], rl[k][:, 0:1], None, ALU.mult, None, [accb, rlb[k]], [osb[k4]])
            DMA(P, "sp", o_d[jt * 128:(jt + 1) * 128, p * 128:(p + 1) * 128], ost[k4][:], [osb[k4]], [], "")

        K.attn_layer(16,
                     lambda p: [(qn[p * 128:(p + 1) * 128, :], 128), (qr[p * 64:(p + 1) * 64, :], 64)],
                     lambda p: [(kng, p * 128, 128, 2048), (krg, 0, 64, 64)],
                     lambda p: (vg, p * 128),
                     128, scale, None, None, None, masks, fin)


def build_p3():
    nc = bass.Bass("TRN2", target_bir_lowering=False)
    qn = dram_in(nc, "qn1", [2048, NT], BF16)
    qr = dram_in(nc, "qr1", [1024, NT], BF16)
    kng = dram_in(nc, "kng", [8 * 2048, NT], BF16)
    krg = dram_in(nc, "krg", [8 * 64, NT], BF16)
    vg = dram_in(nc, "vg", [8 * NT, 2048], BF16)
    masks = dram_in(nc, "masks", [128, 8, 128], F32)
    x = dram_in(nc, "x", [NT, D], F32)
    w_o = dram_in(nc, "w_o", [D, D], F32)
    g1 = dram_in(nc, "g1", [1, D], F32)
    g2 = dram_in(nc, "g2", [1, D], F32)
    g3 = dram_in(nc, "g3", [1, D], F32)
    w_in = dram_in(nc, "w_in", [D, 2 * FFH], F32)
    w_out = dram_in(nc, "w_out", [FFH, D], F32)
    o_d = dram_tmp(nc, "o1", [NT, 2048], BF16)
    x1_d = dram_tmp(nc, "x1", [NT, D], F32)
    h2_d = dram_tmp(nc, "h2", [NT, D], BF16)
    out = dram_out(nc, "out", [NT, D], F32)
    K = KB(nc)
    P = K.P
    mla_attention(K, qn, qr, kng, krg, vg, masks, o_d)
    attn_out(K, o_d, w_o, x, x1_d, h2_d, g1, g2)
    ffn(K, h2_d, x1_d, out, g3, w_in, w_out)
    P.flush()
    P.finish()
    return nc


def build_p2(stage=9):
    nc = bass.Bass("TRN2", target_bir_lowering=False)
    qt = dram_in(nc, "qt0", [2048, NT], BF16)
    ktg = dram_in(nc, "ktg", [8 * 2048, NT], BF16)
    vg = dram_in(nc, "vg", [8 * NT, 2048], BF16)
    posq = dram_in(nc, "posq", [1, NT], I32)
    posk = dram_in(nc, "posk", [128, 128], I32)
    masks = dram_in(nc, "masks", [128, 8, 128], F32)
    lam = dram_in(nc, "lam", [1, 512], F32)
    subln = dram_in(nc, "subln", [1, 256], F32)
    if stage == 1:
        o_d = dram_out(nc, "o0", [NT, 2048], BF16)
    else:
        o_d = dram_tmp(nc, "o0", [NT, 2048], BF16)
    K = KB(nc)
    P = K.P
    global DBG
    if stage == 1:
        DBG = (dram_out(nc, "dbg", [NT, 257], F32), 1)
    if stage >= 2:
        x = dram_in(nc, "x", [NT, D], F32)
        w_o = dram_in(nc, "w_o", [D, D], F32)
        g1 = dram_in(nc, "g1", [1, D], F32)
        g2 = dram_in(nc, "g2", [1, D], F32)
        g3 = dram_in(nc, "g3", [1, D], F32)
        w_in = dram_in(nc, "w_in", [D, 2 * FFH], F32)
        w_out = dram_in(nc, "w_out", [FFH, D], F32)
        x1_d = dram_tmp(nc, "x1", [NT, D], F32)
        h2_d = dram_tmp(nc, "h2", [NT, D], BF16)
        x2_d = dram_out(nc, "x2", [NT, D], F32)
    if stage >= 3:
        g10 = dram_in(nc, "g10", [1, D], F32)
        w_down = dram_in(nc, "w_down", [D, 1088], F32)
        qkn = dram_in(nc, "qkn", [128, 8], F32)
        w_uq = dram_in(nc, "w_uq", [512, 3072], F32)
        w_ukv = dram_in(nc, "w_ukv", [512, 4096], F32)
        ropec = dram_in(nc, "ropec", [64, 3], F32)
        qn_d = dram_out(nc, "qn1", [2048, NT], BF16)
        qr_d = dram_out(nc, "qr1", [1024, NT], BF16)
        kn_d = dram_out(nc, "kn1", [2048, NT], BF16)
        kr_d = dram_out(nc, "kr1", [64, NT], BF16)
        v1_d = dram_out(nc, "v1", [NT, 2048], BF16)
    diff_attention(K, qt, ktg, vg, posq, posk, masks, lam, subln, o_d)
    DBG = None
    if stage >= 2:
        attn_out(K, o_d, w_o, x, x1_d, h2_d, g1, g2)
        ffn(K, h2_d, x1_d, x2_d, g3, w_in, w_out)
    if stage >= 3:
        mla_pre(K, x2_d, g10, w_down, qkn, w_uq, w_ukv, posq, ropec, qn_d, qr_d, kn_d, kr_d, v1_d)
    P.flush()
    P.finish()
    return nc


def make_masks():
    tri = np.where(np.arange(128)[:, None] <= np.arange(128)[None, :], 0.0, NEG).astype(np.float32)
    out = []
    for c in range(NCORE):
        mk = np.zeros((128, 8, 128), np.float32)
        for r in range(8):
            if r == c:
                mk[:, r, :] = tri
            elif r > c:
                mk[:, r, :] = NEG
        out.append(mk)
    return out


def tile_perm():
    idx = np.arange(S).reshape(S // 128, 128)
    per_core = [np.concatenate([idx[8 * j + c] for j in range(NTILE)]) for c in range(NCORE)]
    return per_core


def run(nc, in_maps):
    res = run_bass_kernel_spmd(nc, in_maps, core_ids=list(range(NCORE)))
    return res.results


def all_gather(K, src, dst, ccb):
    P = K.P
    P.dma("pool", lambda e: e.collective_compute("AllGather", ALU.bypass, replica_groups=[list(range(NCORE))],
                                                 ins=[src.opt()], outs=[dst.opt()]),
          [], [ccb], inc=1)


def build_fused():
    nc = bass.Bass("TRN2", target_bir_lowering=False)
    tmp = lambda name, shape, dt: nc.dram_tensor(name, list(shape), dt).ap()
    x = dram_in(nc, "x", [NT, D], F32)
    posq = dram_in(nc, "posq", [1, NT], I32)
    posk = dram_in(nc, "posk", [128, 128], I32)
    masks = dram_in(nc, "masks", [128, 8, 128], F32)
    gains = dram_in(nc, "gains", [8, D], F32)
    w_qkv = dram_in(nc, "w_qkv", [D, 6144], F32)
    lam = dram_in(nc, "lam", [1, 512], F32)
    subln = dram_in(nc, "subln", [1, 256], F32)
    w_o0 = dram_in(nc, "w_o0", [D, D], F32)
    w_in0 = dram_in(nc, "w_in0", [D, 2 * FFH], F32)
    w_out0 = dram_in(nc, "w_out0", [FFH, D], F32)
    w_down = dram_in(nc, "w_down", [D, 1088], F32)
    qkn = dram_in(nc, "qkn", [128, 8], F32)
    w_uq = dram_in(nc, "w_uq", [512, 3072], F32)
    w_ukv = dram_in(nc, "w_ukv", [512, 4096], F32)
    ropec = dram_in(nc, "ropec", [64, 3], F32)
    w_o1 = dram_in(nc, "w_o1", [D, D], F32)
    w_in1 = dram_in(nc, "w_in1", [D, 2 * FFH], F32)
    w_out1 = dram_in(nc, "w_out1", [FFH, D], F32)
    out = dram_out(nc, "out", [NT, D], F32)
    qt0 = tmp("qt0", [2048, NT], BF16)
    kt0 = tmp("kt0", [2048, NT], BF16)
    v0 = tmp("v0", [NT, 2048], BF16)
    ktg = tmp("ktg", [8 * 2048, NT], BF16)
    v0g = tmp("v0g", [8 * NT, 2048], BF16)
    o_d = tmp("o_d", [NT, 2048], BF16)
    x1_d = tmp("x1_d", [NT, D], F32)
    h2_d = tmp("h2_d", [NT, D], BF16)
    x2_d = tmp("x2_d", [NT, D], F32)
    qn1 = tmp("qn1", [2048, NT], BF16)
    qr1 = tmp("qr1", [1024, NT], BF16)
    kn1 = tmp("kn1", [2048, NT], BF16)
    kr1 = tmp("kr1", [64, NT], BF16)
    v1 = tmp("v1", [NT, 2048], BF16)
    kng = tmp("kng", [8 * 2048, NT], BF16)
    krg = tmp("krg", [8 * 64, NT], BF16)
    v1g = tmp("v1g", [8 * NT, 2048], BF16)
    g = lambda i: gains[i:i + 1, :]
    K = KB(nc)
    P = K.P
    qkv_proj(K, x, g(0), w_qkv, qt0, kt0, v0)
    P.flush()
    all_gather(K, kt0, ktg, Buf())
    all_gather(K, v0, v0g, Buf())
    P.flush()
    diff_attention(K, qt0, ktg, v0g, posq, posk, masks, lam, subln, o_d)
    attn_out(K, o_d, w_o0, x, x1_d, h2_d, g(1), g(2))
    ffn(K, h2_d, x1_d, x2_d, g(3), w_in0, w_out0)
    mla_pre(K, x2_d, g(4), w_down, qkn, w_uq, w_ukv, posq, ropec, qn1, qr1, kn1, kr1, v1)
    P.flush()
    all_gather(K, kn1, kng, Buf())
    all_gather(K, kr1, krg, Buf())
    all_gather(K, v1, v1g, Buf())
    P.flush()
    mla_attention(K, qn1, qr1, kng, krg, v1g, masks, o_d)
    attn_out(K, o_d, w_o1, x2_d, x1_d, h2_d, g(5), g(6))
    ffn(K, h2_d, x1_d, out, g(7), w_in1, w_out1)
    P.flush()
    P.finish()
    return nc


def build_p1():
    nc = bass.Bass("TRN2", target_bir_lowering=False)
    x = dram_in(nc, "x", [NT, D], F32)
    g0 = dram_in(nc, "g0", [1, D], F32)
    w = dram_in(nc, "w_qkv", [D, 6144], F32)
    qt = dram_out(nc, "qt0", [2048, NT], BF16)
    kt = dram_out(nc, "kt0", [2048, NT], BF16)
    v = dram_out(nc, "v0", [NT, 2048], BF16)
    K = KB(nc)
    qkv_proj(K, x, g0, w, qt, kt, v)
    K.P.flush()
    K.P.finish()
    return nc


def kernel_unfused(**inputs):
    f32 = np.float32
    x = np.ascontiguousarray(inputs["x"][0], dtype=f32)
    pos = np.ascontiguousarray(inputs["positions"][0]).astype(np.int32)
    ng = np.asarray(inputs["norm_gains"], dtype=f32)
    perm = tile_perm()
    row = lambda v: np.ascontiguousarray(np.asarray(v, dtype=f32).reshape(1, -1))
    xs = [np.ascontiguousarray(x[perm[c]]) for c in range(NCORE)]
    w_qkv = np.ascontiguousarray(inputs["diff_w_qkv"][0], dtype=f32)
    r1 = run(build_p1(), [{"x": xs[c], "g0": row(ng[0, 0]), "w_qkv": w_qkv} for c in range(NCORE)])
    ktg = np.concatenate([r1[c]["kt0"] for c in range(NCORE)], 0)
    vg = np.concatenate([r1[c]["v0"] for c in range(NCORE)], 0)
    masks = make_masks()
    posk = np.ascontiguousarray(pos.reshape(NTILE, NCORE, 128).transpose(2, 1, 0).reshape(128, 128))
    invf = (10000.0 ** (-np.arange(0, 64, 2, dtype=np.float64) / 64)).astype(f32)
    ropec = np.zeros((64, 3), f32)
    ropec[:, 0] = np.concatenate([invf, invf]) / TWO_PI
    ropec[:, 1] = 0.25
    ropec[:32, 2] = 0.5
    qkn = np.ascontiguousarray(np.concatenate([np.asarray(inputs["mla_q_norm"][0], f32),
                                               np.asarray(inputs["mla_kv_norm"][0], f32)]).reshape(8, 128).T)
    in2 = []
    for c in range(NCORE):
        in2.append({"qt0": r1[c]["qt0"], "ktg": ktg, "vg": vg,
                    "posq": np.ascontiguousarray(pos[perm[c]][None, :]), "posk": posk, "masks": masks[c],
                    "lam": row(inputs["diff_lambda"][0]), "subln": row(inputs["diff_subln"][0]),
                    "x": xs[c], "w_o": np.ascontiguousarray(inputs["diff_w_o"][0], dtype=f32),
                    "g1": row(ng[0, 1]), "g2": row(ng[0, 2]), "g3": row(ng[0, 3]),
                    "w_in": np.ascontiguousarray(inputs["ffn_w_in"][0], dtype=f32),
                    "w_out": np.ascontiguousarray(inputs["ffn_w_out"][0], dtype=f32),
                    "g10": row(ng[1, 0]), "w_down": np.ascontiguousarray(inputs["mla_w_down"][0], dtype=f32),
                    "qkn": qkn, "w_uq": np.ascontiguousarray(inputs["mla_w_uq"][0], dtype=f32),
                    "w_ukv": np.ascontiguousarray(inputs["mla_w_ukv"][0], dtype=f32), "ropec": ropec})
    r2 = run(build_p2(3), in2)
    del r1, ktg, vg
    kng = np.concatenate([r2[c]["kn1"] for c in range(NCORE)], 0)
    krg = np.concatenate([r2[c]["kr1"] for c in range(NCORE)], 0)
    v1g = np.concatenate([r2[c]["v1"] for c in range(NCORE)], 0)
    in3 = []
    for c in range(NCORE):
        in3.append({"qn1": r2[c]["qn1"], "qr1": r2[c]["qr1"], "kng": kng, "krg": krg, "vg": v1g,
                    "masks": masks[c], "x": np.ascontiguousarray(r2[c]["x2"]),
                    "w_o": np.ascontiguousarray(inputs["mla_w_o"][0], dtype=f32),
                    "g1": row(ng[1, 1]), "g2": row(ng[1, 2]), "g3": row(ng[1, 3]),
                    "w_in": np.ascontiguousarray(inputs["ffn_w_in"][1], dtype=f32),
                    "w_out": np.ascontiguousarray(inputs["ffn_w_out"][1], dtype=f32)})
    r3 = run(build_p3(), in3)
    out = np.empty((S, D), f32)
    for c in range(NCORE):
        out[perm[c]] = r3[c]["out"]
    return out[None]


def kernel_fused(**inputs):
    f32 = np.float32
    x = np.ascontiguousarray(inputs["x"][0], dtype=f32)
    pos = np.ascontiguousarray(inputs["positions"][0]).astype(np.int32)
    ng = np.asarray(inputs["norm_gains"], dtype=f32)
    perm = tile_perm()
    row = lambda v: np.ascontiguousarray(np.asarray(v, dtype=f32).reshape(1, -1))
    c32 = lambda v: np.ascontiguousarray(np.asarray(v, dtype=f32))
    masks = make_masks()
    posk = np.ascontiguousarray(pos.reshape(NTILE, NCORE, 128).transpose(2, 1, 0).reshape(128, 128))
    invf = (10000.0 ** (-np.arange(0, 64, 2, dtype=np.float64) / 64)).astype(f32)
    ropec = np.zeros((64, 3), f32)
    ropec[:, 0] = np.concatenate([invf, invf]) / TWO_PI
    ropec[:, 1] = 0.25
    ropec[:32, 2] = 0.5
    qkn = np.ascontiguousarray(np.concatenate([np.asarray(inputs["mla_q_norm"][0], f32),
                                               np.asarray(inputs["mla_kv_norm"][0], f32)]).reshape(8, 128).T)
    shared = {"posk": posk, "gains": c32(ng.reshape(8, D)), "w_qkv": c32(inputs["diff_w_qkv"][0]),
              "lam": row(inputs["diff_lambda"][0]), "subln": row(inputs["diff_subln"][0]),
              "w_o0": c32(inputs["diff_w_o"][0]), "w_in0": c32(inputs["ffn_w_in"][0]),
              "w_out0": c32(inputs["ffn_w_out"][0]), "w_down": c32(inputs["mla_w_down"][0]), "qkn": qkn,
              "w_uq": c32(inputs["mla_w_uq"][0]), "w_ukv": c32(inputs["mla_w_ukv"][0]), "ropec": ropec,
              "w_o1": c32(inputs["mla_w_o"][0]), "w_in1": c32(inputs["ffn_w_in"][1]),
              "w_out1": c32(inputs["ffn_w_out"][1])}
    in_maps = []
    for c in range(NCORE):
        m = dict(shared)
        m["x"] = np.ascontiguousarray(x[perm[c]])
        m["posq"] = np.ascontiguousarray(pos[perm[c]][None, :])
        m["masks"] = masks[c]
        in_maps.append(m)
    res = run(build_fused(), in_maps)
    out = np.empty((S, D), f32)
    for c in range(NCORE):
        out[perm[c]] = res[c]["out"]
    return out[None]


FUSED = True
kernel = kernel_fused if FUSED else kernel_unfused
```

```python
import contextlib
import math
import numpy as np
import ml_dtypes
import concourse.bass as bass
import concourse.mybir as mybir
from concourse.bass_utils import run_bass_kernel_spmd

F32 = mybir.dt.float32
BF16 = mybir.dt.bfloat16
I32 = mybir.dt.int32
ALU = mybir.AluOpType
AF = mybir.ActivationFunctionType

NCORE = 8
S = 16384
D = 2048
NT = S // NCORE
NTILE = NT // 128
FFH = 5632
EPS = 1e-6
NEG = -30000.0
ENGS = ("pe", "act", "dve", "pool", "sp")


class Buf:
    __slots__ = ("name", "w", "r", "sem")

    def __init__(self, name=""):
        self.name = name
        self.w = None
        self.r = []
        self.sem = None


class Prog:
    def __init__(self, nc, n_dma_sems=60):
        self.nc = nc
        self.q = {e: [] for e in ENGS}
        self.cnt = {e: 0 for e in ENGS}
        self.gen = {e: 0 for e in ENGS}
        self.seen = {e: {} for e in ENGS}
        self.n_dma_sems = n_dma_sems
        self.streams = {}
        self.dcount = [0] * n_dma_sems
        self.sem_free = list(range(n_dma_sems))
        self.sem_scopes = [[]]
        self.stack = contextlib.ExitStack()
        self.esem = {}
        self.dsem = []
        self.ninst = 0

    def sb(self, name, shape, dt):
        self.uid = getattr(self, "uid", 0) + 1
        return self.stack.enter_context(self.nc.sbuf_tensor("%s_u%d" % (name, self.uid), list(shape), dt))

    def ps(self, name, shape, dt):
        self.uid = getattr(self, "uid", 0) + 1
        return self.stack.enter_context(self.nc.psum_tensor("%s_u%d" % (name, self.uid), list(shape), dt))

    def scope(self):
        return _Scope(self)

    def _deps(self, eng, reads, writes):
        need = {}
        for b in reads:
            if b.w is not None:
                k, v = b.w
                if need.get(k, 0) < v:
                    need[k] = v
        for b in writes:
            if b.w is not None:
                k, v = b.w
                if need.get(k, 0) < v:
                    need[k] = v
            for (k, v) in b.r:
                if need.get(k, 0) < v:
                    need[k] = v
        waits = []
        seen = self.seen[eng]
        for k, v in need.items():
            if k[0] == "e":
                if k[2] < self.gen[k[1]]:
                    continue
                if k[1] == "pe" and eng == "pe":
                    continue
            if seen.get(k, 0) >= v:
                continue
            seen[k] = v
            waits.append((k, v))
        return waits

    def _mark(self, tok, reads, writes):
        for b in reads:
            b.r.append(tok)
            if len(b.r) > 48:
                m = {}
                for k, v in b.r:
                    if m.get(k, 0) < v:
                        m[k] = v
                b.r = list(m.items())
        for b in writes:
            b.w = tok
            b.r = []

    def op(self, eng, fn, reads=(), writes=()):
        waits = self._deps(eng, reads, writes)
        self.cnt[eng] += 1
        key = ("e", eng, self.gen[eng])
        tok = (key, self.cnt[eng])
        self.q[eng].append((waits, fn, key, 1))
        self._mark(tok, reads, writes)
        self.ninst += 1 + len(waits)
        return tok

    def dma(self, eng, fn, reads=(), writes=(), stream=None, inc=16):
        waits = self._deps(eng, reads, writes)
        owner = writes[0] if len(writes) else reads[0]
        if owner.sem is None:
            assert self.sem_free, "out of dma sems"
            owner.sem = self.sem_free.pop(0)
            self.sem_scopes[-1].append((owner.sem, owner))
        idx = owner.sem
        self.dcount[idx] += inc
        key = ("d", idx)
        tok = (key, self.dcount[idx])
        self.q[eng].append((waits, fn, key, inc))
        self._mark(tok, reads, writes)
        self.ninst += 1 + len(waits)
        return tok

    def _sem(self, key):
        if key[0] == "d":
            return self.dsem[key[1]]
        assert key[2] == self.gen[key[1]]
        return self.esem[key[1]]

    def start(self):
        nc = self.nc
        self.semstack = contextlib.ExitStack()
        self.ectx = {}
        for e in ENGS:
            self.ectx[e] = nc.semaphore("s_" + e)
            self.esem[e] = self.ectx[e].__enter__()
        for i in range(self.n_dma_sems):
            self.dsem.append(self.semstack.enter_context(nc.semaphore("d%d" % i)))

    def finish(self):
        self.stack.close()
        for e in ENGS:
            self.ectx[e].__exit__(None, None, None)
        self.semstack.close()

    def flush(self):
        nc = self.nc
        waits = []
        for idx, c in enumerate(self.dcount):
            k = ("d", idx)
            if c > 0 and self.seen["sp"].get(k, 0) < c:
                self.seen["sp"][k] = c
                waits.append((k, c))
        if waits:
            self.q["sp"].append((waits, None, None, 0))
        with nc.Block() as block:
            regs = {"pe": block.tensor, "act": block.scalar, "dve": block.vector,
                    "pool": block.gpsimd, "sp": block.sync}
            for e in ENGS:
                items = self.q[e]
                if not items:
                    continue

                def body(engine, items=items):
                    for waits, fn, skey, inc in items:
                        for k, v in waits:
                            engine.wait_ge(self._sem(k), v)
                        if fn is not None:
                            ins = fn(engine)
                            if inc == 1 and skey[0] == "d":
                                ins.then_inc(self._sem(skey))
                            else:
                                ins.then_inc(self._sem(skey), inc)
                regs[e](body)
        big = [e2 for e2 in ENGS if self.cnt[e2] > 8000]
        if big:
            with nc.Block() as block:
                def clr(engine, big=big):
                    for e2 in big:
                        engine.sem_clear(self.esem[e2])
                block.gpsimd(clr)
            for e2 in big:
                self.gen[e2] += 1
                self.cnt[e2] = 0
        for e in ENGS:
            for e2 in ENGS:
                self.seen[e][("e", e2, self.gen[e2])] = self.cnt[e2]
            for idx, c in enumerate(self.dcount):
                self.seen[e][("d", idx)] = c
        self.q = {e: [] for e in ENGS}


class _Scope:
    def __init__(self, prog):
        self.prog = prog

    def __enter__(self):
        self.saved = self.prog.stack
        self.prog.stack = contextlib.ExitStack()
        self.prog.sem_scopes.append([])
        return self

    def __exit__(self, *a):
        self.prog.flush()
        self.prog.stack.close()
        self.prog.stack = self.saved
        for idx, owner in self.prog.sem_scopes.pop():
            owner.sem = None
            self.prog.sem_free.append(idx)
        return False


def MM(P, out, lhsT, rhs, start, stop, reads, writes):
    P.op("pe", lambda e: e.matmul(out, lhsT=lhsT, rhs=rhs, start=start, stop=stop), reads, writes)


def TR(P, out, in_, ident, reads, writes):
    P.op("pe", lambda e: e.transpose(out=out, in_=in_, identity=ident), reads, writes)


def ACTV(P, out, in_, func, reads, writes, scale=None, bias=None, accum=None):
    kw = {}
    if scale is not None:
        kw["scale"] = scale
    if bias is not None:
        kw["bias"] = bias
    if accum is not None:
        kw["accum_out"] = accum
    P.op("act", lambda e: e.activation(out=out, in_=in_, func=func, **kw), reads, writes)


def DMA(P, q, out, in_, reads, writes, stream):
    P.dma(q, lambda e: e.dma_start(out=out, in_=in_), reads, writes, stream)


def TS(P, eng, out, in0, s1, s2, op0, op1, reads, writes):
    if s2 is None:
        P.op(eng, lambda e: e.tensor_scalar(out=out, in0=in0, scalar1=s1, scalar2=None, op0=op0), reads, writes)
    else:
        P.op(eng, lambda e: e.tensor_scalar(out=out, in0=in0, scalar1=s1, scalar2=s2, op0=op0, op1=op1), reads, writes)


def STT(P, out, in0, scalar, in1, op0, op1, reads, writes):
    P.op("dve", lambda e: e.scalar_tensor_tensor(out=out, in0=in0, scalar=scalar, in1=in1, op0=op0, op1=op1),
         reads, writes)


def TT(P, eng, out, in0, in1, op, reads, writes):
    P.op(eng, lambda e: e.tensor_tensor(out=out, in0=in0, in1=in1, op=op), reads, writes)


def CP(P, eng, out, in_, reads, writes):
    if eng == "act":
        P.op("act", lambda e: e.copy(out=out, in_=in_), reads, writes)
    else:
        P.op(eng, lambda e: e.tensor_copy(out=out, in_=in_), reads, writes)


def MEMSET(P, eng, ap, val, writes):
    P.op(eng, lambda e: e.memset(ap, val), (), writes)


class KB:
    def __init__(self, nc):
        self.nc = nc
        self.P = Prog(nc)
        self.P.start()
        P = self.P
        self.ident = P.sb("ident", [128, 128], BF16)
        self.identb = Buf("ident")
        self.neghalf = P.sb("neghalf", [128, 1], F32)
        self.cb = Buf("consts")
        ident = self.ident
        MEMSET(P, "pool", ident[:], 0.0, [self.identb])
        P.op("pool", lambda e: e.affine_select(out=ident[:], in_=ident[:], pattern=[[-1, 128]],
                                               compare_op=ALU.not_equal, fill=1.0, base=0,
                                               channel_multiplier=1), [self.identb], [self.identb])
        MEMSET(P, "pool", self.neghalf[:], -0.5, [self.cb])
        self.uid = 0

    def name(self, s):
        self.uid += 1
        return "%s_%d" % (s, self.uid)

    def rstd(self, out, outb, ss, ssb, n):
        P = self.P
        TS(P, "pool", out, ss, 1.0 / n, EPS, ALU.mult, ALU.add, [ssb], [outb])
        TT(P, "pool", out, out, self.neghalf[:], ALU.pow, [outb, self.cb], [outb])

    def norm_T(self, src, gsrc, dstT, dstB, ntiles, src_tile0=0, dst_col0=0, Dm=D):
        P = self.P
        KC = Dm // 128
        with P.scope():
            gbc = P.sb(self.name("gbc"), [128, Dm], F32)
            gb = Buf()
            DMA(P, "sp", gbc[:], gsrc.partition_broadcast(128), [], [gb], "nt_g")
            xt = [P.sb(self.name("xt"), [128, Dm], F32) for _ in range(2)]
            xb = [Buf() for _ in range(2)]
            junk = P.sb(self.name("junk"), [128, Dm], BF16)
            jb = Buf()
            hn = [P.sb(self.name("hn"), [128, Dm], BF16) for _ in range(2)]
            hb = [Buf() for _ in range(2)]
            ss = [P.sb(self.name("ss"), [128, 1], F32) for _ in range(2)]
            sb_ = [Buf() for _ in range(2)]
            rs = [P.sb(self.name("rs"), [128, 1], F32) for _ in range(2)]
            rb = [Buf() for _ in range(2)]
            pt = [P.ps(self.name("pt"), [128, 8, 128], BF16) for _ in range(2)]
            pb = [Buf() for _ in range(2)]
            for t in range(ntiles):
                s = t % 2
                r0 = (src_tile0 + t) * 128
                DMA(P, "sp", xt[s][:], src[r0:r0 + 128, :], [], [xb[s]], "nt_x%d" % s)
                ACTV(P, junk[:], xt[s][:], AF.Square, [xb[s]], [jb, sb_[s]], accum=ss[s][:])
                self.rstd(rs[s][:], rb[s], ss[s][:], sb_[s], Dm)
                STT(P, hn[s][:], xt[s][:], rs[s][:, 0:1], gbc[:], ALU.mult, ALU.mult,
                    [xb[s], rb[s], gb], [hb[s]])
                for g in range(KC // 8):
                    k = (t * (KC // 8) + g) % 2
                    for j in range(8):
                        c = g * 8 + j
                        TR(P, pt[k][:, j, :], hn[s][:, c * 128:(c + 1) * 128], self.ident[:],
                           [hb[s], self.identb], [pb[k]])
                    col = dst_col0 + t * 128
                    CP(P, "act" if g % 2 == 0 else "dve", dstT[:, g * 8:(g + 1) * 8, col:col + 128], pt[k][:],
                       [pb[k]], [dstB])

    def load_w(self, dst, dstb, wsrc, stream):
        DMA(self.P, "pool", dst, wsrc.rearrange("(kc p) f -> p kc f", p=128), [], [dstb], stream)

    def proj_F(self, AT, ATb, tok0, ntok, W, blocks, KC, evac, wbufs, psF):
        P = self.P
        ci = 0
        k = 0
        for bi, (c0, n) in enumerate(blocks):
            wt, wb = wbufs[bi % len(wbufs)]
            self.load_w(wt[:, :, 0:n], wb, W[:, c0:c0 + n], "wF%d" % (bi % len(wbufs)))
            for sub in range(n // 128):
                for tb in range(ntok // 512):
                    ps, pb = psF[k % len(psF)]
                    k += 1
                    for kc in range(KC):
                        MM(P, ps[:, 0:512], wt[:, kc, sub * 128:(sub + 1) * 128],
                           AT[:, kc, tok0 + tb * 512: tok0 + (tb + 1) * 512], kc == 0, kc == KC - 1,
                           [wb, ATb], [pb])
                    evac(ci, tb, ps, pb)
                ci += 1

    def proj_T(self, AT, ATb, tok0, ntiles, W, blocks, KC, evac, wbufs, psT):
        P = self.P
        k = 0
        for bi, (c0, n) in enumerate(blocks):
            wt, wb = wbufs[bi % len(wbufs)]
            self.load_w(wt[:, :, 0:n], wb, W[:, c0:c0 + n], "wT%d" % (bi % len(wbufs)))
            for t in range(ntiles):
                ps, pb = psT[k % len(psT)]
                k += 1
                for kc in range(KC):
                    MM(P, ps[:, 0:n], AT[:, kc, tok0 + t * 128: tok0 + (t + 1) * 128], wt[:, kc, 0:n],
                       kc == 0, kc == KC - 1, [wb, ATb], [pb])
                evac(bi, t, ps, pb)

    def attn_layer(self, npass, qparts, kparts, vsrc, dv, scale, slopes, posq, posk, masks_src, fin):
        P = self.P
        nparts = len(qparts(0))
        with P.scope():
            KT = []
            for j in range(nparts):
                rows = kparts(0)[j][2]
                KT.append(P.sb(self.name("KT"), [rows, 32, 512], BF16))
            Cb = [Buf() for _ in range(32)]
            KTb = [Cb for _ in range(nparts)]
            V = P.sb(self.name("V"), [128, 32, 4, dv + 1], BF16)
            Vb = Cb
            for ch in range(32):
                MEMSET(P, "pool", V[:, ch, :, dv:dv + 1], 1.0, [Vb[ch]])
            msk = P.sb(self.name("msk"), [128, 8, 128], F32)
            mb = Buf()
            DMA(P, "sp", msk[:], masks_src, [], [mb], "at_m")
            QT = [[P.sb(self.name("QT"), [qparts(0)[j][1], 512], BF16) for j in range(nparts)] for _ in range(2)]
            QTb = [[b_] * nparts for b_ in (Buf(), Buf())]
            LA = 3
            NB = 4
            psS = [P.ps(self.name("psS"), [128, 512], F32) for _ in range(NB)]
            psSb = [Buf() for _ in range(NB)]
            acc = [P.ps(self.name("acc"), [128, 512], F32) for _ in range(4)]
            accb = [Buf() for _ in range(4)]
            pT = [P.sb(self.name("pT"), [128, 512], BF16) for _ in range(NB)]
            pTb = [Buf() for _ in range(NB)]
            if slopes is not None:
                tt = [P.sb(self.name("tt"), [128, 512], F32) for _ in range(NB)]
                ttb = [Buf() for _ in range(NB)]
                pqi = P.sb(self.name("pqi"), [128, NT], I32)
                pqf = P.sb(self.name("pqf"), [128, NT], F32)
                pqb = Buf()
                DMA(P, "sp", pqi[:], posq.partition_broadcast(128), [], [pqb], "at_p")
                CP(P, "dve", pqf[:], pqi[:], [pqb], [pqb])
                pki = P.sb(self.name("pki"), [128, 128], I32)
                pkf = P.sb(self.name("pkf"), [128, 128], F32)
                pkb = Buf()
                DMA(P, "sp", pki[:], posk, [], [pkb], "at_p2")
                CP(P, "dve", pkf[:], pki[:], [pkb], [pkb])
                kbias = [P.sb(self.name("kbias"), [128, 128], F32) for _ in range(2)]
                kbb = [Buf() for _ in range(2)]
                R = [P.sb(self.name("R"), [128, 512], F32) for _ in range(2)]
                Rb = [Buf() for _ in range(2)]
                Rm = [P.sb(self.name("Rm"), [128, 32, 128], F32) for _ in range(2)]
                Rmb = [Buf() for _ in range(2)]
            it = 0

            def load_chunk(p, r, lc):
                kp = kparts(p)
                vs, vc0 = vsrc(p)
                ch = r * 4 + lc
                for j in range(nparts):
                    src, rb_, rows, rpr = kp[j]
                    DMA(P, "sp", KT[j][:, ch, :],
                        src[r * rpr + rb_: r * rpr + rb_ + rows, lc * 512:(lc + 1) * 512],
                        [], [KTb[j][ch]], "")
                DMA(P, "sp", V[:, ch, :, 0:dv],
                    vs[r * NT + lc * 512: r * NT + (lc + 1) * 512, vc0:vc0 + dv].rearrange(
                        "(t p) d -> p t d", p=128),
                    [], [Vb[ch]], "")

            def prep_group(gi):
                p, m = gi // 4, gi % 4
                qq = gi % 2
                qp = qparts(p)
                for j in range(nparts):
                    DMA(P, "sp", QT[qq][j][:], qp[j][0][:, m * 512:(m + 1) * 512], [], [QTb[qq][j]], "")
                if slopes is not None:
                    sl = slopes[p]
                    kk = p % 2
                    if m == 0:
                        TS(P, "pool", kbias[kk][:], pkf[:], sl / scale, None, ALU.mult, None, [pkb], [kbb[kk]])
                    TS(P, "pool", R[qq][:], pqf[:, m * 512:(m + 1) * 512], -sl / scale, None, ALU.mult, None,
                       [pqb], [Rb[qq]])
                    for ip in range(4):
                        for r in range(8):
                            TT(P, "pool", Rm[qq][:, ip * 8 + r, :], R[qq][:, ip * 128:(ip + 1) * 128],
                               msk[:, r, :], ALU.add, [Rb[qq], mb], [Rmb[qq]])

            for lc in range(4):
                for r in range(8):
                    load_chunk(0, r, lc)
            prep_group(0)
            ngroups = npass * 4
            for gi in range(ngroups):
                p, m = gi // 4, gi % 4
                qq = gi % 2
                kk = p % 2
                if gi + 1 < ngroups:
                    prep_group(gi + 1)
                if True:
                    tiles = []
                    reload_after = {}
                    for r in range(8):
                        for lt in range(4 * m):
                            tiles.append((r * 4 + lt // 4, lt % 4, -1, r))
                        if m == 3 and p + 1 < npass:
                            reload_after[len(tiles) - 1] = [(r, lc) for lc in range(3)]
                    for ip in range(4):
                        for r in range(8):
                            tiles.append((r * 4 + m, ip, ip, r))
                    if m == 3 and p + 1 < npass:
                        reload_after[len(tiles) - 1] = [(r, 3) for r in range(8)]
                    ntl = len(tiles)
                    pendq = []
                    for ti in range(ntl + LA):
                        if ti < ntl:
                            ch, tl, ip, r = tiles[ti]
                            q0 = 0 if ip < 0 else ip * 128
                            n = 512 - q0
                            sb_i = it % NB
                            it += 1
                            ps, pb = psS[sb_i], psSb[sb_i]
                            for j in range(nparts):
                                MM(P, ps[:, 0:n], KT[j][:, ch, tl * 128:(tl + 1) * 128], QT[qq][j][:, q0:512],
                                   j == 0, j == nparts - 1, [KTb[j][ch], QTb[qq][j]], [pb])
                            if slopes is not None:
                                ktcol = r * 16 + (ch % 4) * 4 + tl
                                if ip < 0:
                                    STT(P, tt[sb_i][:, 0:n], ps[:, 0:n], kbias[kk][:, ktcol:ktcol + 1],
                                        R[qq][:, q0:512], ALU.add, ALU.add, [pb, kbb[kk], Rb[qq]], [ttb[sb_i]])
                                else:
                                    STT(P, tt[sb_i][:, 0:128], ps[:, 0:128], kbias[kk][:, ktcol:ktcol + 1],
                                        Rm[qq][:, ip * 8 + r, :], ALU.add, ALU.add, [pb, kbb[kk], Rmb[qq]],
                                        [ttb[sb_i]])
                                    if n > 128:
                                        STT(P, tt[sb_i][:, 128:n], ps[:, 128:n], kbias[kk][:, ktcol:ktcol + 1],
                                            R[qq][:, q0 + 128:512], ALU.add, ALU.add, [pb, kbb[kk], Rb[qq]],
                                            [ttb[sb_i]])
                                ACTV(P, pT[sb_i][:, 0:n], tt[sb_i][:, 0:n], AF.Exp, [ttb[sb_i]], [pTb[sb_i]],
                                     scale=scale)
                            else:
                                if ip >= 0:
                                    TT(P, "dve", ps[:, 0:128], ps[:, 0:128], msk[:, r, :], ALU.add, [pb, mb], [pb])
                                ACTV(P, pT[sb_i][:, 0:n], ps[:, 0:n], AF.Exp, [pb], [pTb[sb_i]], scale=scale)
                            pendq.append((sb_i, ch, tl, ip, r, ti))
                        if ti >= LA:
                            sbp, chp, tlp, ipp, rp, tip = pendq.pop(0)
                            i0 = 0 if ipp < 0 else ipp
                            for i in range(i0, 4):
                                first = (tip == 0)
                                last = (ipp == i and rp == 7)
                                MM(P, acc[i][:, 0:dv + 1], pT[sbp][:, (i - i0) * 128:(i - i0 + 1) * 128],
                                   V[:, chp, tlp, 0:dv + 1], first, last, [pTb[sbp], Vb[chp]], [accb[i]])
                                if last:
                                    fin(p, m, i, acc[i], accb[i])
                            for (r_, lc_) in reload_after.get(tip, ()):
                                load_chunk(p + 1, r_, lc_)
                if m == 3 and (p + 1) % 4 == 0 and p + 1 < npass:
                    P.flush()


def dram_in(nc, name, shape, dt):
    return nc.dram_tensor(name, list(shape), dt, kind="ExternalInput").ap()


def dram_out(nc, name, shape, dt):
    return nc.dram_tensor(name, list(shape), dt, kind="ExternalOutput").ap()


def qkv_proj(K, x, g0, w, qt, kt, v):
    P = K.P
    with P.scope():
        hT = P.sb("hT", [128, 16, NT], BF16)
        hTb = Buf()
        K.norm_T(x, g0, hT, hTb, NTILE)
        with P.scope():
            wbufs = [(P.sb("wb%d" % i, [128, 16, 512], BF16), Buf()) for i in range(2)]
            psF = [(P.ps("psF%d" % i, [128, 512], F32), Buf()) for i in range(4)]
            stg = [(P.sb("stg%d" % i, [128, 512], BF16), Buf()) for i in range(4)]
            cnt = [0]

            def evac_qk(ci, tb, ps, pb):
                k = cnt[0] % 4
                cnt[0] += 1
                st, sb_ = stg[k]
                CP(P, "act" if k % 2 == 0 else "dve", st[:], ps[:, 0:512], [pb], [sb_])
                dst = qt if ci < 16 else kt
                r0 = (ci % 16) * 128
                DMA(P, "sp", dst[r0:r0 + 128, tb * 512:(tb + 1) * 512], st[:], [sb_], [], "")

            K.proj_F(hT, hTb, 0, NT, w, [(c * 512, 512) for c in range(8)], 16, evac_qk, wbufs, psF)

            def evac_v(bi, t, ps, pb):
                k = cnt[0] % 4
                cnt[0] += 1
                st, sb_ = stg[k]
                CP(P, "act" if k % 2 == 0 else "dve", st[:], ps[:, 0:512], [pb], [sb_])
                DMA(P, "sp", v[t * 128:(t + 1) * 128, bi * 512:(bi + 1) * 512], st[:], [sb_], [], "")

            K.proj_T(hT, hTb, 0, NTILE, w, [(4096 + c * 512, 512) for c in range(4)], 16, evac_v, wbufs, psF)


def dram_tmp(nc, name, shape, dt):
    return nc.dram_tensor(name, list(shape), dt, kind="Internal").ap()


DBG = None
LAMBDA_INIT0 = 0.8 - 0.6 * math.exp(-0.3 * 0)


def diff_attention(K, qt, ktg, vg, posq, posk, masks, lam, subln, o_d):
    P = K.P
    scale = 128 ** -0.5
    with P.scope():
        lamt = P.sb("lamt", [128, 4, 128], F32)
        lb = Buf()
        DMA(P, "sp", lamt[:], lam.partition_broadcast(128), [], [lb], "da_c")
        lj = P.sb("lj", [128, 128], F32)
        s12 = P.sb("s12", [128, 2], F32)
        e12 = P.sb("e12", [128, 2], F32)
        neglam = P.sb("neglam", [128, 1], F32)
        nlb = Buf()
        for i in range(2):
            P.op("dve", lambda e, i=i: e.scalar_tensor_tensor(out=lj[:], in0=lamt[:, 2 * i, :], scalar=1.0,
                                                              in1=lamt[:, 2 * i + 1, :], op0=ALU.mult, op1=ALU.mult,
                                                              accum_out=s12[:, i:i + 1]), [lb], [lb])
        ACTV(P, e12[:], s12[:], AF.Exp, [lb], [lb])
        TT(P, "dve", neglam[:], e12[:, 1:2], e12[:, 0:1], ALU.subtract, [lb], [nlb])
        TS(P, "dve", neglam[:], neglam[:], -LAMBDA_INIT0, None, ALU.add, None, [nlb], [nlb])
        gsub = P.sb("gsub", [128, 256], F32)
        gsb = Buf()
        DMA(P, "sp", gsub[:], subln.partition_broadcast(128), [], [gsb], "da_c")
        TS(P, "dve", gsub[:], gsub[:], 1.0 - LAMBDA_INIT0, None, ALU.mult, None, [gsb], [gsb])
        O1 = P.sb("O1", [128, NTILE, 256], F32)
        O1b = [Buf() for _ in range(NTILE)]
        rl = [P.sb("rl%d" % i, [128, 1], F32) for i in range(2)]
        rlb = [Buf() for _ in range(2)]
        dt_ = [P.sb("dt%d" % i, [128, 256], F32) for i in range(2)]
        dtb = [Buf() for _ in range(2)]
        dj = P.sb("dj", [128, 256], F32)
        djb = Buf()
        ssd = [P.sb("ssd%d" % i, [128, 1], F32) for i in range(2)]
        ssb = [Buf() for _ in range(2)]
        rsd = [P.sb("rsd%d" % i, [128, 1], F32) for i in range(2)]
        rsb = [Buf() for _ in range(2)]
        ost = [P.sb("ost%d" % i, [128, 256], BF16) for i in range(4)]
        osb = [Buf() for _ in range(4)]
        cnt = [0]

        def fin(p, m, i, acc, accb):
            h, s = p // 2, p % 2
            jt = 4 * m + i
            k = cnt[0] % 2
            k4 = cnt[0] % 4
            cnt[0] += 1
            if DBG is not None and p == DBG[1]:
                dbt = P.sb(K.name("dbt"), [128, 257], F32)
                dbb = Buf()
                CP(P, "dve", dbt[:], acc[:, 0:257], [accb], [dbb])
                DMA(P, "sp", DBG[0][jt * 128:(jt + 1) * 128, :], dbt[:], [dbb], [], "dbg")
            P.op("dve", lambda e: e.reciprocal(out=rl[k][:], in_=acc[:, 256:257]), [accb], [rlb[k]])
            if s == 0:
                TS(P, "dve", O1[:, jt, :], acc[:, 0:256], rl[k][:, 0:1], None, ALU.mult, None,
                   [accb, rlb[k]], [O1b[jt]])
            else:
                TT(P, "dve", rl[k][:], rl[k][:], neglam[:], ALU.mult, [rlb[k], nlb], [rlb[k]])
                STT(P, dt_[k][:], acc[:, 0:256], rl[k][:, 0:1], O1[:, jt, :], ALU.mult, ALU.add,
                    [accb, rlb[k], O1b[jt]], [dtb[k]])
                P.op("dve", lambda e: e.scalar_tensor_tensor(out=dj[:], in0=dt_[k][:], scalar=1.0, in1=dt_[k][:],
                                                             op0=ALU.mult, op1=ALU.mult, accum_out=ssd[k][:]),
                     [dtb[k]], [djb, ssb[k]])
                K.rstd(rsd[k][:], rsb[k], ssd[k][:], ssb[k], 256)
                STT(P, ost[k4][:], dt_[k][:], rsd[k][:, 0:1], gsub[:], ALU.mult, ALU.mult,
                    [dtb[k], rsb[k], gsb], [osb[k4]])
                DMA(P, "sp", o_d[jt * 128:(jt + 1) * 128, h * 256:(h + 1) * 256], ost[k4][:], [osb[k4]], [],
                    "da_o%d" % k4)

        slopes = [2.0 ** (-(p // 2 + 1)) for p in range(16)]
        K.attn_layer(16,
                     lambda p: [(qt[p * 128:(p + 1) * 128, :], 128)],
                     lambda p: [(ktg, p * 128, 128, 2048)],
                     lambda p: (vg, (p // 2) * 256),
                     256, scale, slopes, posq, posk, masks, fin)


def loadT(K, src, tile0, ntiles, dstT, dstB, Dm=D):
    P = K.P
    KC = Dm // 128
    with P.scope():
        ot = [P.sb(K.name("lt_o"), [128, Dm], BF16) for _ in range(2)]
        ob = [Buf() for _ in range(2)]
        pt = [P.ps(K.name("lt_p"), [128, 8, 128], BF16) for _ in range(2)]
        pb = [Buf() for _ in range(2)]
        for t in range(ntiles):
            s = t % 2
            r0 = (tile0 + t) * 128
            DMA(P, "sp", ot[s][:], src[r0:r0 + 128, :], [], [ob[s]], "")
            for g in range(KC // 8):
                k = (t * (KC // 8) + g) % 2
                for j in range(8):
                    c = g * 8 + j
                    TR(P, pt[k][:, j, :], ot[s][:, c * 128:(c + 1) * 128], K.ident[:], [ob[s], K.identb], [pb[k]])
                CP(P, "act" if g % 2 == 0 else "dve", dstT[:, g * 8:(g + 1) * 8, t * 128:(t + 1) * 128], pt[k][:],
                   [pb[k]], [dstB])


def attn_out(K, o_d, w_o, x_in, x1_d, h2_d, g1, g2):
    P = K.P
    with P.scope():
        Wo = P.sb("Wo", [128, 16, 2048], BF16)
        Wob = [Buf() for _ in range(4)]
        for c in range(4):
            K.load_w(Wo[:, :, c * 512:(c + 1) * 512], Wob[c], w_o[:, c * 512:(c + 1) * 512], "")
        gbc1 = P.sb("gbc1", [128, D], F32)
        gbc2 = P.sb("gbc2", [128, D], F32)
        g1b, g2b = Buf(), Buf()
        DMA(P, "sp", gbc1[:], g1.partition_broadcast(128), [], [g1b], "")
        DMA(P, "sp", gbc2[:], g2.partition_broadcast(128), [], [g2b], "")
        ot = [P.sb("ao_o%d" % i, [128, D], BF16) for i in range(2)]
        ob = [Buf() for _ in range(2)]
        OT = [P.sb("ao_T%d" % i, [128, 16, 128], BF16) for i in range(2)]
        OTb = [Buf() for _ in range(2)]
        xt = [P.sb("ao_x%d" % i, [128, D], F32) for i in range(2)]
        xb = [Buf() for _ in range(2)]
        y1 = P.sb("ao_y", [128, D], F32)
        yb = Buf()
        hn = [P.sb("ao_h%d" % i, [128, D], BF16) for i in range(2)]
        hb = [Buf() for _ in range(2)]
        junk = P.sb("ao_j", [128, D], BF16)
        jb = Buf()
        ss = [P.sb("ao_ss%d" % i, [128, 4], F32) for i in range(2)]
        ssb = [Buf() for _ in range(2)]
        rs = [P.sb("ao_rs%d" % i, [128, 2], F32) for i in range(2)]
        rsb = [Buf() for _ in range(2)]
        psY = [P.ps("ao_py%d" % i, [128, 512], F32) for i in range(4)]
        pyb = [Buf() for _ in range(4)]
        pt = [P.ps("ao_pt%d" % i, [128, 8, 128], BF16) for i in range(2)]
        ptb = [Buf() for _ in range(2)]
        for t in range(NTILE):
            s = t % 2
            r0 = t * 128
            DMA(P, "sp", ot[s][:], o_d[r0:r0 + 128, :], [], [ob[s]], "")
            DMA(P, "sp", xt[s][:], x_in[r0:r0 + 128, :], [], [xb[s]], "")
            for g in range(2):
                k = g
                for j in range(8):
                    c = g * 8 + j
                    TR(P, pt[k][:, j, :], ot[s][:, c * 128:(c + 1) * 128], K.ident[:], [ob[s], K.identb], [ptb[k]])
                CP(P, "act" if g == 0 else "dve", OT[s][:, g * 8:(g + 1) * 8, :], pt[k][:], [ptb[k]], [OTb[s]])
            for cb in range(4):
                for kc in range(16):
                    MM(P, psY[cb][:, :], OT[s][:, kc, :], Wo[:, kc, cb * 512:(cb + 1) * 512], kc == 0, kc == 15,
                       [OTb[s], Wob[cb]], [pyb[cb]])
                ACTV(P, junk[:, cb * 512:(cb + 1) * 512], psY[cb][:, :], AF.Square, [pyb[cb]], [jb, ssb[s]],
                     accum=ss[s][:, cb:cb + 1])
            TT(P, "pool", ss[s][:, 0:2], ss[s][:, 0:2], ss[s][:, 2:4], ALU.add, [ssb[s]], [ssb[s]])
            TT(P, "pool", ss[s][:, 0:1], ss[s][:, 0:1], ss[s][:, 1:2], ALU.add, [ssb[s]], [ssb[s]])
            K.rstd(rs[s][:, 0:1], rsb[s], ss[s][:, 0:1], ssb[s], D)
            for cb in range(4):
                STT(P, y1[:, cb * 512:(cb + 1) * 512], psY[cb][:, :], rs[s][:, 0:1], gbc1[:, cb * 512:(cb + 1) * 512],
                    ALU.mult, ALU.mult, [pyb[cb], rsb[s], g1b], [yb])
            TT(P, "pool", xt[s][:], xt[s][:], y1[:], ALU.add, [xb[s], yb], [xb[s]])
            DMA(P, "sp", x1_d[r0:r0 + 128, :], xt[s][:], [xb[s]], [], "")
            ACTV(P, junk[:], xt[s][:], AF.Square, [xb[s]], [jb, ssb[s]], accum=ss[s][:, 3:4])
            K.rstd(rs[s][:, 1:2], rsb[s], ss[s][:, 3:4], ssb[s], D)
            STT(P, hn[s][:], xt[s][:], rs[s][:, 1:2], gbc2[:], ALU.mult, ALU.mult, [xb[s], rsb[s], g2b], [hb[s]])
            DMA(P, "sp", h2_d[r0:r0 + 128, :], hn[s][:], [hb[s]], [], "")


def ffn(K, h2_d, x1_d, x_out, g3, w_in, w_out):
    P = K.P
    for half in range(2):
        with P.scope():
            actT = P.sb("actT", [128, 44, 1024], BF16)
            actb = Buf()
            with P.scope():
                h2T = P.sb("h2T", [128, 16, 1024], BF16)
                h2Tb = Buf()
                loadT(K, h2_d, 8 * half, 8, h2T, h2Tb)
                wg = [P.sb("wg%d" % i, [128, 16, 256], BF16) for i in range(2)]
                wu = [P.sb("wu%d" % i, [128, 16, 256], BF16) for i in range(2)]
                wgb = [Buf() for _ in range(2)]
                wub = [Buf() for _ in range(2)]
                sg = [P.sb("sg%d" % i, [128, 512], F32) for i in range(2)]
                sgb = [Buf() for _ in range(2)]
                psG = [P.ps("psG%d" % i, [128, 512], F32) for i in range(2)]
                psU = [P.ps("psU%d" % i, [128, 512], F32) for i in range(2)]
                pgb = [Buf() for _ in range(2)]
                pub = [Buf() for _ in range(2)]
                cnt = 0
                for jb in range(22):
                    sl = jb % 2
                    K.load_w(wg[sl][:], wgb[sl], w_in[:, jb * 256:(jb + 1) * 256], "")
                    K.load_w(wu[sl][:], wub[sl], w_in[:, FFH + jb * 256:FFH + (jb + 1) * 256], "")
                    for sub in range(2):
                        j = 2 * jb + sub
                        for tb in range(2):
                            k = cnt % 2
                            cnt += 1
                            for kc in range(16):
                                MM(P, psG[k][:, :], wg[sl][:, kc, sub * 128:(sub + 1) * 128],
                                   h2T[:, kc, tb * 512:(tb + 1) * 512], kc == 0, kc == 15, [wgb[sl], h2Tb], [pgb[k]])
                            for kc in range(16):
                                MM(P, psU[k][:, :], wu[sl][:, kc, sub * 128:(sub + 1) * 128],
                                   h2T[:, kc, tb * 512:(tb + 1) * 512], kc == 0, kc == 15, [wub[sl], h2Tb], [pub[k]])
                            ACTV(P, sg[k][:], psG[k][:, :], AF.Silu, [pgb[k]], [sgb[k]])
                            TT(P, "dve", actT[:, j, tb * 512:(tb + 1) * 512], sg[k][:], psU[k][:, :], ALU.mult,
                               [sgb[k], pub[k]], [actb])
            with P.scope():
                wo = [P.sb("wo%d" % i, [128, 44, 256], BF16) for i in range(2)]
                wob = [Buf() for _ in range(2)]
                ysb = P.sb("ysb", [128, 4, D], F32)
                yb = [Buf() for _ in range(4)]
                xt = [P.sb("ff_x%d" % i, [128, D], F32) for i in range(2)]
                xb = [Buf() for _ in range(2)]
                gbc3 = P.sb("gbc3", [128, D], F32)
                g3b = Buf()
                DMA(P, "sp", gbc3[:], g3.partition_broadcast(128), [], [g3b], "")
                junk = P.sb("ff_j", [128, D], BF16)
                jb_ = Buf()
                ss = [P.sb("ff_ss%d" % i, [128, 1], F32) for i in range(2)]
                ssb = [Buf() for _ in range(2)]
                rs = [P.sb("ff_rs%d" % i, [128, 1], F32) for i in range(2)]
                rsb = [Buf() for _ in range(2)]
                psO = [P.ps("psO%d" % i, [128, 512], F32) for i in range(2)]
                pob = [Buf() for _ in range(2)]
                cnt = 0
                wl = 0
                for quarter in range(2):
                    for cb in range(8):
                        sl = wl % 2
                        wl += 1
                        DMA(P, "pool", wo[sl][:], w_out[:, cb * 256:(cb + 1) * 256].rearrange("(j p) c -> p j c", p=128),
                            [], [wob[sl]], "")
                        for t in range(4):
                            c0 = quarter * 512 + t * 128
                            k = cnt % 2
                            cnt += 1
                            for j in range(44):
                                MM(P, psO[k][:, 0:256], actT[:, j, c0:c0 + 128], wo[sl][:, j, :], j == 0, j == 43,
                                   [actb, wob[sl]], [pob[k]])
                            CP(P, "act" if k == 0 else "dve", ysb[:, t, cb * 256:(cb + 1) * 256], psO[k][:, 0:256],
                               [pob[k]], [yb[t]])
                    for t in range(4):
                        s = t % 2
                        r0 = (half * 8 + quarter * 4 + t) * 128
                        DMA(P, "sp", xt[s][:], x1_d[r0:r0 + 128, :], [], [xb[s]], "")
                        ACTV(P, junk[:], ysb[:, t, :], AF.Square, [yb[t]], [jb_, ssb[s]], accum=ss[s][:])
                        K.rstd(rs[s][:], rsb[s], ss[s][:], ssb[s], D)
                        STT(P, ysb[:, t, :], ysb[:, t, :], rs[s][:, 0:1], gbc3[:], ALU.mult, ALU.mult,
                            [yb[t], rsb[s], g3b], [yb[t]])
                        TT(P, "pool", xt[s][:], xt[s][:], ysb[:, t, :], ALU.add, [xb[s], yb[t]], [xb[s]])
                        DMA(P, "sp", x_out[r0:r0 + 128, :], xt[s][:], [xb[s]], [], "")


TWO_PI = 2.0 * math.pi


def mla_pre(K, x_d, g0, w_down, qkn, w_uq, w_ukv, posq, ropec, qn_d, qr_d, kn_d, kr_d, v_d):
    P = K.P
    with P.scope():
        hT = P.sb("m_hT", [128, 16, NT], BF16)
        hTb = Buf()
        K.norm_T(x_d, g0, hT, hTb, NTILE)
        cn = P.sb("m_cn", [128, 8, NT], BF16)
        cnb = Buf()
        cosT = P.sb("m_cos", [64, NT], F32)
        sinT = P.sb("m_sin", [64, NT], F32)
        tabb = Buf()
        with P.scope():
            pqi = P.sb("m_pqi", [64, NT], I32)
            pqf = P.sb("m_pqf", [64, NT], F32)
            pb_ = Buf()
            DMA(P, "sp", pqi[:], posq.partition_broadcast(64), [], [pb_], "")
            CP(P, "dve", pqf[:], pqi[:], [pb_], [pb_])
            rc = P.sb("m_rc", [64, 3], F32)
            rcb = Buf()
            DMA(P, "sp", rc[:], ropec, [], [rcb], "")
            v = P.sb("m_v", [64, NT], F32)
            vi = P.sb("m_vi", [64, NT], I32)
            vf = P.sb("m_vf", [64, NT], F32)
            vb = Buf()
            for dst, col in ((cosT, 1), (sinT, 2)):
                TS(P, "dve", v[:], pqf[:], rc[:, 0:1], rc[:, col:col + 1], ALU.mult, ALU.add, [pb_, rcb], [vb])
                CP(P, "dve", vi[:], v[:], [vb], [vb])
                CP(P, "dve", vf[:], vi[:], [vb], [vb])
                TT(P, "dve", v[:], v[:], vf[:], ALU.subtract, [vb], [vb])
                STT(P, vf[:], v[:], 0.5, v[:], ALU.is_gt, ALU.subtract, [vb], [vb])
                ACTV(P, dst[:], vf[:], AF.Sin, [vb], [tabb], scale=-TWO_PI)
        stg = [(P.sb("m_stg%d" % i, [128, 512], BF16), Buf()) for i in range(4)]
        rt = [(P.sb("m_rt%d" % i, [64, 2, 512], F32), Buf()) for i in range(2)]
        cnt = [0]

        def evac_to(dst_d, r0, rows, tb, ps, pb):
            k = cnt[0] % 4
            cnt[0] += 1
            st, sb_ = stg[k]
            CP(P, "act" if k % 2 == 0 else "dve", st[0:rows, :], ps[0:rows, 0:512], [pb], [sb_])
            DMA(P, "sp", dst_d[r0:r0 + rows, tb * 512:(tb + 1) * 512], st[0:rows, :], [sb_], [], "")

        def rope_out(dst_d, r0, tb, psA, pab, psB, pbb):
            k = cnt[0] % 4
            cnt[0] += 1
            st, sb_ = stg[k]
            r_, rb_ = rt[k % 2]
            TT(P, "dve", r_[:, 0, :], psA[0:64, 0:512], cosT[:, tb * 512:(tb + 1) * 512], ALU.mult, [pab, tabb], [rb_])
            TT(P, "dve", r_[:, 1, :], psB[0:64, 0:512], sinT[:, tb * 512:(tb + 1) * 512], ALU.mult, [pbb, tabb], [rb_])
            TT(P, "pool", st[0:64, :], r_[:, 0, :], r_[:, 1, :], ALU.add, [rb_], [sb_])
            DMA(P, "sp", dst_d[r0:r0 + 64, tb * 512:(tb + 1) * 512], st[0:64, :], [sb_], [], "")

        with P.scope():
            wbufs = [(P.sb("m_wb%d" % i, [128, 16, 512], BF16), Buf()) for i in range(2)]
            psC = [(P.ps("m_psC%d" % i, [128, 512], F32), Buf()) for i in range(4)]
            psN = [(P.ps("m_psN%d" % i, [128, 512], F32), Buf()) for i in range(2)]
            psR = [(P.ps("m_psR%d" % i, [128, 512], F32), Buf()) for i in range(2)]
            wr = P.sb("m_wr", [128, 16, 64], BF16)
            ws = P.sb("m_ws", [128, 16, 64], BF16)
            wrb = Buf()
            wsb = Buf()
            K.load_w(wr[:], wrb, w_down[:, 1024:1088], "")
            K.load_w(ws[:, :, 0:32], wsb, w_down[:, 1056:1088], "")
            K.load_w(ws[:, :, 32:64], wsb, w_down[:, 1024:1056], "")
            for tb in range(4):
                (psA, pab), (psB, pbb) = psR[0], psR[1]
                for kc in range(16):
                    MM(P, psA[0:64, 0:512], wr[:, kc, :], hT[:, kc, tb * 512:(tb + 1) * 512], kc == 0, kc == 15,
                       [wrb, hTb], [pab])
                for kc in range(16):
                    MM(P, psB[0:64, 0:512], ws[:, kc, :], hT[:, kc, tb * 512:(tb + 1) * 512], kc == 0, kc == 15,
                       [wsb, hTb], [pbb])
                rope_out(kr_d, 0, tb, psA, pab, psB, pbb)
            ones = P.sb("m_ones", [128, 128], BF16)
            onb = Buf()
            MEMSET(P, "pool", ones[:], 1.0, [onb])
            nh = P.sb("m_nh", [128, 512], F32)
            nhb = Buf()
            MEMSET(P, "pool", nh[:], -0.5, [nhb])
            gn = P.sb("m_gn", [128, 8], F32)
            gnb = Buf()
            DMA(P, "sp", gn[:], qkn, [], [gnb], "")
            sq = [(P.sb("m_sq%d" % i, [128, 512], BF16), Buf()) for i in range(4)]
            vv = [(P.sb("m_vv%d" % i, [128, 512], F32), Buf()) for i in range(2)]
            it = 0
            for half in range(2):
                wt, wb = wbufs[half]
                K.load_w(wt[:], wb, w_down[:, half * 512:(half + 1) * 512], "")
                for tb in range(4):
                    pn, pnb = psN[it % 2]
                    vt, vtb = vv[it % 2]
                    it += 1
                    for ci in range(4):
                        ps, pb = psC[ci]
                        for kc in range(16):
                            MM(P, ps[:, 0:512], wt[:, kc, ci * 128:(ci + 1) * 128], hT[:, kc, tb * 512:(tb + 1) * 512],
                               kc == 0, kc == 15, [wb, hTb], [pb])
                        ACTV(P, sq[ci][0][:], ps[:, 0:512], AF.Square, [pb], [sq[ci][1]])
                    for ci in range(4):
                        MM(P, pn[:, 0:512], ones[:], sq[ci][0][:], ci == 0, ci == 3, [onb, sq[ci][1]], [pnb])
                    TS(P, "dve", vt[:], pn[:, 0:512], 1.0 / 512, EPS, ALU.mult, ALU.add, [pnb], [vtb])
                    TT(P, "pool", vt[:], vt[:], nh[:], ALU.pow, [vtb, nhb], [vtb])
                    for ci in range(4):
                        c = half * 4 + ci
                        ps, pb = psC[ci]
                        STT(P, cn[:, c, tb * 512:(tb + 1) * 512], ps[:, 0:512], gn[:, c:c + 1],
                            vt[:], ALU.mult, ALU.mult, [pb, gnb, vtb], [cnb])
        with P.scope():
            wq = P.sb("m_wq", [128, 4, 3072], BF16)
            wqb = Buf()
            for c in range(6):
                K.load_w(wq[:, :, c * 512:(c + 1) * 512], wqb, w_uq[:, c * 512:(c + 1) * 512], "")
            wqs = P.sb("m_wqs", [128, 4, 16, 64], BF16)
            wqsb = Buf()
            uq3 = w_uq.rearrange("(kc p) (h d) -> kc p h d", p=128, d=192)
            for kc in range(4):
                DMA(P, "pool", wqs[:, kc, :, 0:32], uq3[kc, :, :, 160:192], [], [wqsb], "")
                DMA(P, "pool", wqs[:, kc, :, 32:64], uq3[kc, :, :, 128:160], [], [wqsb], "")
            psQ = [(P.ps("m_psQ%d" % i, [128, 512], F32), Buf()) for i in range(6)]
            it = 0
            for h in range(16):
                for tb in range(4):
                    (ps, pb), (psA, pab), (psB, pbb) = psQ[3 * (it % 2)], psQ[3 * (it % 2) + 1], psQ[3 * (it % 2) + 2]
                    it += 1
                    for kc in range(4):
                        MM(P, ps[:, 0:512], wq[:, kc, h * 192:h * 192 + 128], cn[:, kc, tb * 512:(tb + 1) * 512],
                           kc == 0, kc == 3, [wqb, cnb], [pb])
                    for kc in range(4):
                        MM(P, psA[0:64, 0:512], wq[:, kc, h * 192 + 128:h * 192 + 192],
                           cn[:, kc, tb * 512:(tb + 1) * 512], kc == 0, kc == 3, [wqb, cnb], [pab])
                    for kc in range(4):
                        MM(P, psB[0:64, 0:512], wqs[:, kc, h, :], cn[:, kc, tb * 512:(tb + 1) * 512],
                           kc == 0, kc == 3, [wqsb, cnb], [pbb])
                    evac_to(qn_d, h * 128, 128, tb, ps, pb)
                    rope_out(qr_d, h * 64, tb, psA, pab, psB, pbb)
        with P.scope():
            wkn = P.sb("m_wkn", [128, 4, 16, 128], BF16)
            wv = P.sb("m_wv", [128, 4, 16, 128], BF16)
            wkb = Buf()
            wvb = Buf()
            kv3 = w_ukv.rearrange("(kc p) (h d) -> kc p h d", p=128, d=256)
            for kc in range(4):
                DMA(P, "pool", wkn[:, kc, :, :], kv3[kc, :, :, 0:128], [], [wkb], "")
                DMA(P, "pool", wv[:, kc, :, :], kv3[kc, :, :, 128:256], [], [wvb], "")
            psK = [(P.ps("m_psK%d" % i, [128, 512], F32), Buf()) for i in range(4)]
            it = 0
            for h in range(16):
                for tb in range(4):
                    ps, pb = psK[it % 4]
                    it += 1
                    for kc in range(4):
                        MM(P, ps[:, 0:512], wkn[:, kc, h, :], cn[:, 4 + kc, tb * 512:(tb + 1) * 512],
                           kc == 0, kc == 3, [wkb, cnb], [pb])
                    evac_to(kn_d, h * 128, 128, tb, ps, pb)
            for t in range(NTILE):
                for hb in range(4):
                    ps, pb = psK[it % 4]
                    it += 1
                    for kc in range(4):
                        MM(P, ps[:, 0:512], cn[:, 4 + kc, t * 128:(t + 1) * 128],
                           wv[:, kc, 4 * hb:4 * hb + 4, :].rearrange("p h d -> p (h d)"),
                           kc == 0, kc == 3, [wvb, cnb], [pb])
                    k = cnt[0] % 4
                    cnt[0] += 1
                    st, sb_ = stg[k]
                    CP(P, "act" if k % 2 == 0 else "dve", st[:], ps[:, 0:512], [pb], [sb_])
                    DMA(P, "sp", v_d[t * 128:(t + 1) * 128, hb * 512:(hb + 1) * 512], st[:], [sb_], [], "")


def mla_attention(K, qn, qr, kng, krg, vg, masks, o_d):
    P = K.P
    scale = 192 ** -0.5
    with P.scope():
        rl = [P.sb("ma_rl%d" % i, [128, 1], F32) for i in range(2)]
        rlb = [Buf() for _ in range(2)]
        ost = [P.sb("ma_o%d" % i, [128, 128], BF16) for i in range(4)]
        osb = [Buf() for _ in range(4)]
        cnt = [0]

        def fin(p, m, i, acc, accb):
            jt = 4 * m + i
            k = cnt[0] % 2
            k4 = cnt[0] % 4
            cnt[0] += 1
            P.op("dve", lambda e: e.reciprocal(out=rl[k][:], in_=acc[:, 128:129]), [accb], [rlb[k]])
            TS(P, "dve", ost[k4][:], acc[:, 0:128], rl[k][:, 0:1], None, ALU.mult, None, [accb, rlb[k]], [osb[k4]])
            DMA(P, "sp", o_d[jt * 128:(jt + 1) * 128, p * 128:(p + 1) * 128], ost[k4][:], [osb[k4]], [], "")

        K.attn_layer(16,
                     lambda p: [(qn[p * 128:(p + 1) * 128, :], 128), (qr[p * 64:(p + 1) * 64, :], 64)],
                     lambda p: [(kng, p * 128, 128, 2048), (krg, 0, 64, 64)],
                     lambda p: (vg, p * 128),
                     128, scale, None, None, None, masks, fin)


def build_p3():
    nc = bass.Bass("TRN2", target_bir_lowering=False)
    qn = dram_in(nc, "qn1", [2048, NT], BF16)
    qr = dram_in(nc, "qr1", [1024, NT], BF16)
    kng = dram_in(nc, "kng", [8 * 2048, NT], BF16)
    krg = dram_in(nc, "krg", [8 * 64, NT], BF16)
    vg = dram_in(nc, "vg", [8 * NT, 2048], BF16)
    masks = dram_in(nc, "masks", [128, 8, 128], F32)
    x = dram_in(nc, "x", [NT, D], F32)
    w_o = dram_in(nc, "w_o", [D, D], F32)
    g1 = dram_in(nc, "g1", [1, D], F32)
    g2 = dram_in(nc, "g2", [1, D], F32)
    g3 = dram_in(nc, "g3", [1, D], F32)
    w_in = dram_in(nc, "w_in", [D, 2 * FFH], F32)
    w_out = dram_in(nc, "w_out", [FFH, D], F32)
    o_d = dram_tmp(nc, "o1", [NT, 2048], BF16)
    x1_d = dram_tmp(nc, "x1", [NT, D], F32)
    h2_d = dram_tmp(nc, "h2", [NT, D], BF16)
    out = dram_out(nc, "out", [NT, D], F32)
    K = KB(nc)
    P = K.P
    mla_attention(K, qn, qr, kng, krg, vg, masks, o_d)
    attn_out(K, o_d, w_o, x, x1_d, h2_d, g1, g2)
    ffn(K, h2_d, x1_d, out, g3, w_in, w_out)
    P.flush()
    P.finish()
    return nc


def build_p2(stage=9):
    nc = bass.Bass("TRN2", target_bir_lowering=False)
    qt = dram_in(nc, "qt0", [2048, NT], BF16)
    ktg = dram_in(nc, "ktg", [8 * 2048, NT], BF16)
    vg = dram_in(nc, "vg", [8 * NT, 2048], BF16)
    posq = dram_in(nc, "posq", [1, NT], I32)
    posk = dram_in(nc, "posk", [128, 128], I32)
    masks = dram_in(nc, "masks", [128, 8, 128], F32)
    lam = dram_in(nc, "lam", [1, 512], F32)
    subln = dram_in(nc, "subln", [1, 256], F32)
    if stage == 1:
        o_d = dram_out(nc, "o0", [NT, 2048], BF16)
    else:
        o_d = dram_tmp(nc, "o0", [NT, 2048], BF16)
    K = KB(nc)
    P = K.P
    global DBG
    if stage == 1:
        DBG = (dram_out(nc, "dbg", [NT, 257], F32), 1)
    if stage >= 2:
        x = dram_in(nc, "x", [NT, D], F32)
        w_o = dram_in(nc, "w_o", [D, D], F32)
        g1 = dram_in(nc, "g1", [1, D], F32)
        g2 = dram_in(nc, "g2", [1, D], F32)
        g3 = dram_in(nc, "g3", [1, D], F32)
        w_in = dram_in(nc, "w_in", [D, 2 * FFH], F32)
        w_out = dram_in(nc, "w_out", [FFH, D], F32)
        x1_d = dram_tmp(nc, "x1", [NT, D], F32)
        h2_d = dram_tmp(nc, "h2", [NT, D], BF16)
        x2_d = dram_out(nc, "x2", [NT, D], F32)
    if stage >= 3:
        g10 = dram_in(nc, "g10", [1, D], F32)
        w_down = dram_in(nc, "w_down", [D, 1088], F32)
        qkn = dram_in(nc, "qkn", [128, 8], F32)
        w_uq = dram_in(nc, "w_uq", [512, 3072], F32)
        w_ukv = dram_in(nc, "w_ukv", [512, 4096], F32)
        ropec = dram_in(nc, "ropec", [64, 3], F32)
        qn_d = dram_out(nc, "qn1", [2048, NT], BF16)
        qr_d = dram_out(nc, "qr1", [1024, NT], BF16)
        kn_d = dram_out(nc, "kn1", [2048, NT], BF16)
        kr_d = dram_out(nc, "kr1", [64, NT], BF16)
        v1_d = dram_out(nc, "v1", [NT, 2048], BF16)
    diff_attention(K, qt, ktg, vg, posq, posk, masks, lam, subln, o_d)
    DBG = None
    if stage >= 2:
        attn_out(K, o_d, w_o, x, x1_d, h2_d, g1, g2)
        ffn(K, h2_d, x1_d, x2_d, g3, w_in, w_out)
    if stage >= 3:
        mla_pre(K, x2_d, g10, w_down, qkn, w_uq, w_ukv, posq, ropec, qn_d, qr_d, kn_d, kr_d, v1_d)
    P.flush()
    P.finish()
    return nc


def make_masks():
    tri = np.where(np.arange(128)[:, None] <= np.arange(128)[None, :], 0.0, NEG).astype(np.float32)
    out = []
    for c in range(NCORE):
        mk = np.zeros((128, 8, 128), np.float32)
        for r in range(8):
            if r == c:
                mk[:, r, :] = tri
            elif r > c:
                mk[:, r, :] = NEG
        out.append(mk)
    return out


def tile_perm():
    idx = np.arange(S).reshape(S // 128, 128)
    per_core = [np.concatenate([idx[8 * j + c] for j in range(NTILE)]) for c in range(NCORE)]
    return per_core


def run(nc, in_maps):
    res = run_bass_kernel_spmd(nc, in_maps, core_ids=list(range(NCORE)))
    return res.results


def all_gather(K, src, dst, ccb):
    P = K.P
    P.dma("pool", lambda e: e.collective_compute("AllGather", ALU.bypass, replica_groups=[list(range(NCORE))],
                                                 ins=[src.opt()], outs=[dst.opt()]),
          [], [ccb], inc=1)


def build_fused():
    nc = bass.Bass("TRN2", target_bir_lowering=False)
    tmp = lambda name, shape, dt: nc.dram_tensor(name, list(shape), dt).ap()
    x = dram_in(nc, "x", [NT, D], F32)
    posq = dram_in(nc, "posq", [1, NT], I32)
    posk = dram_in(nc, "posk", [128, 128], I32)
    masks = dram_in(nc, "masks", [128, 8, 128], F32)
    gains = dram_in(nc, "gains", [8, D], F32)
    w_qkv = dram_in(nc, "w_qkv", [D, 6144], F32)
    lam = dram_in(nc, "lam", [1, 512], F32)
    subln = dram_in(nc, "subln", [1, 256], F32)
    w_o0 = dram_in(nc, "w_o0", [D, D], F32)
    w_in0 = dram_in(nc, "w_in0", [D, 2 * FFH], F32)
    w_out0 = dram_in(nc, "w_out0", [FFH, D], F32)
    w_down = dram_in(nc, "w_down", [D, 1088], F32)
    qkn = dram_in(nc, "qkn", [128, 8], F32)
    w_uq = dram_in(nc, "w_uq", [512, 3072], F32)
    w_ukv = dram_in(nc, "w_ukv", [512, 4096], F32)
    ropec = dram_in(nc, "ropec", [64, 3], F32)
    w_o1 = dram_in(nc, "w_o1", [D, D], F32)
    w_in1 = dram_in(nc, "w_in1", [D, 2 * FFH], F32)
    w_out1 = dram_in(nc, "w_out1", [FFH, D], F32)
    out = dram_out(nc, "out", [NT, D], F32)
    qt0 = tmp("qt0", [2048, NT], BF16)
    kt0 = tmp("kt0", [2048, NT], BF16)
    v0 = tmp("v0", [NT, 2048], BF16)
    ktg = tmp("ktg", [8 * 2048, NT], BF16)
    v0g = tmp("v0g", [8 * NT, 2048], BF16)
    o_d = tmp("o_d", [NT, 2048], BF16)
    x1_d = tmp("x1_d", [NT, D], F32)
    h2_d = tmp("h2_d", [NT, D], BF16)
    x2_d = tmp("x2_d", [NT, D], F32)
    qn1 = tmp("qn1", [2048, NT], BF16)
    qr1 = tmp("qr1", [1024, NT], BF16)
    kn1 = tmp("kn1", [2048, NT], BF16)
    kr1 = tmp("kr1", [64, NT], BF16)
    v1 = tmp("v1", [NT, 2048], BF16)
    kng = tmp("kng", [8 * 2048, NT], BF16)
    krg = tmp("krg", [8 * 64, NT], BF16)
    v1g = tmp("v1g", [8 * NT, 2048], BF16)
    g = lambda i: gains[i:i + 1, :]
    K = KB(nc)
    P = K.P
    qkv_proj(K, x, g(0), w_qkv, qt0, kt0, v0)
    P.flush()
    all_gather(K, kt0, ktg, Buf())
    all_gather(K, v0, v0g, Buf())
    P.flush()
    diff_attention(K, qt0, ktg, v0g, posq, posk, masks, lam, subln, o_d)
    attn_out(K, o_d, w_o0, x, x1_d, h2_d, g(1), g(2))
    ffn(K, h2_d, x1_d, x2_d, g(3), w_in0, w_out0)
    mla_pre(K, x2_d, g(4), w_down, qkn, w_uq, w_ukv, posq, ropec, qn1, qr1, kn1, kr1, v1)
    P.flush()
    all_gather(K, kn1, kng, Buf())
    all_gather(K, kr1, krg, Buf())
    all_gather(K, v1, v1g, Buf())
    P.flush()
    mla_attention(K, qn1, qr1, kng, krg, v1g, masks, o_d)
    attn_out(K, o_d, w_o1, x2_d, x1_d, h2_d, g(5), g(6))
    ffn(K, h2_d, x1_d, out, g(7), w_in1, w_out1)
    P.flush()
    P.finish()
    return nc


def build_p1():
    nc = bass.Bass("TRN2", target_bir_lowering=False)
    x = dram_in(nc, "x", [NT, D], F32)
    g0 = dram_in(nc, "g0", [1, D], F32)
    w = dram_in(nc, "w_qkv", [D, 6144], F32)
    qt = dram_out(nc, "qt0", [2048, NT], BF16)
    kt = dram_out(nc, "kt0", [2048, NT], BF16)
    v = dram_out(nc, "v0", [NT, 2048], BF16)
    K = KB(nc)
    qkv_proj(K, x, g0, w, qt, kt, v)
    K.P.flush()
    K.P.finish()
    return nc


def kernel_unfused(**inputs):
    f32 = np.float32
    x = np.ascontiguousarray(inputs["x"][0], dtype=f32)
    pos = np.ascontiguousarray(inputs["positions"][0]).astype(np.int32)
    ng = np.asarray(inputs["norm_gains"], dtype=f32)
    perm = tile_perm()
    row = lambda v: np.ascontiguousarray(np.asarray(v, dtype=f32).reshape(1, -1))
    xs = [np.ascontiguousarray(x[perm[c]]) for c in range(NCORE)]
    w_qkv = np.ascontiguousarray(inputs["diff_w_qkv"][0], dtype=f32)
    r1 = run(build_p1(), [{"x": xs[c], "g0": row(ng[0, 0]), "w_qkv": w_qkv} for c in range(NCORE)])
    ktg = np.concatenate([r1[c]["kt0"] for c in range(NCORE)], 0)
    vg = np.concatenate([r1[c]["v0"] for c in range(NCORE)], 0)
    masks = make_masks()
    posk = np.ascontiguousarray(pos.reshape(NTILE, NCORE, 128).transpose(2, 1, 0).reshape(128, 128))
    invf = (10000.0 ** (-np.arange(0, 64, 2, dtype=np.float64) / 64)).astype(f32)
    ropec = np.zeros((64, 3), f32)
    ropec[:, 0] = np.concatenate([invf, invf]) / TWO_PI
    ropec[:, 1] = 0.25
    ropec[:32, 2] = 0.5
    qkn = np.ascontiguousarray(np.concatenate([np.asarray(inputs["mla_q_norm"][0], f32),
                                               np.asarray(inputs["mla_kv_norm"][0], f32)]).reshape(8, 128).T)
    in2 = []
    for c in range(NCORE):
        in2.append({"qt0": r1[c]["qt0"], "ktg": ktg, "vg": vg,
                    "posq": np.ascontiguousarray(pos[perm[c]][None, :]), "posk": posk, "masks": masks[c],
                    "lam": row(inputs["diff_lambda"][0]), "subln": row(inputs["diff_subln"][0]),
                    "x": xs[c], "w_o": np.ascontiguousarray(inputs["diff_w_o"][0], dtype=f32),
                    "g1": row(ng[0, 1]), "g2": row(ng[0, 2]), "g3": row(ng[0, 3]),
                    "w_in": np.ascontiguousarray(inputs["ffn_w_in"][0], dtype=f32),
                    "w_out": np.ascontiguousarray(inputs["ffn_w_out"][0], dtype=f32),
                    "g10": row(ng[1, 0]), "w_down": np.ascontiguousarray(inputs["mla_w_down"][0], dtype=f32),
                    "qkn": qkn, "w_uq": np.ascontiguousarray(inputs["mla_w_uq"][0], dtype=f32),
                    "w_ukv": np.ascontiguousarray(inputs["mla_w_ukv"][0], dtype=f32), "ropec": ropec})
    r2 = run(build_p2(3), in2)
    del r1, ktg, vg
    kng = np.concatenate([r2[c]["kn1"] for c in range(NCORE)], 0)
    krg = np.concatenate([r2[c]["kr1"] for c in range(NCORE)], 0)
    v1g = np.concatenate([r2[c]["v1"] for c in range(NCORE)], 0)
    in3 = []
    for c in range(NCORE):
        in3.append({"qn1": r2[c]["qn1"], "qr1": r2[c]["qr1"], "kng": kng, "krg": krg, "vg": v1g,
                    "masks": masks[c], "x": np.ascontiguousarray(r2[c]["x2"]),
                    "w_o": np.ascontiguousarray(inputs["mla_w_o"][0], dtype=f32),
                    "g1": row(ng[1, 1]), "g2": row(ng[1, 2]), "g3": row(ng[1, 3]),
                    "w_in": np.ascontiguousarray(inputs["ffn_w_in"][1], dtype=f32),
                    "w_out": np.ascontiguousarray(inputs["ffn_w_out"][1], dtype=f32)})
    r3 = run(build_p3(), in3)
    out = np.empty((S, D), f32)
    for c in range(NCORE):
        out[perm[c]] = r3[c]["out"]
    return out[None]


def kernel_fused(**inputs):
    f32 = np.float32
    x = np.ascontiguousarray(inputs["x"][0], dtype=f32)
    pos = np.ascontiguousarray(inputs["positions"][0]).astype(np.int32)
    ng = np.asarray(inputs["norm_gains"], dtype=f32)
    perm = tile_perm()
    row = lambda v: np.ascontiguousarray(np.asarray(v, dtype=f32).reshape(1, -1))
    c32 = lambda v: np.ascontiguousarray(np.asarray(v, dtype=f32))
    masks = make_masks()
    posk = np.ascontiguousarray(pos.reshape(NTILE, NCORE, 128).transpose(2, 1, 0).reshape(128, 128))
    invf = (10000.0 ** (-np.arange(0, 64, 2, dtype=np.float64) / 64)).astype(f32)
    ropec = np.zeros((64, 3), f32)
    ropec[:, 0] = np.concatenate([invf, invf]) / TWO_PI
    ropec[:, 1] = 0.25
    ropec[:32, 2] = 0.5
    qkn = np.ascontiguousarray(np.concatenate([np.asarray(inputs["mla_q_norm"][0], f32),
                                               np.asarray(inputs["mla_kv_norm"][0], f32)]).reshape(8, 128).T)
    shared = {"posk": posk, "gains": c32(ng.reshape(8, D)), "w_qkv": c32(inputs["diff_w_qkv"][0]),
              "lam": row(inputs["diff_lambda"][0]), "subln": row(inputs["diff_subln"][0]),
              "w_o0": c32(inputs["diff_w_o"][0]), "w_in0": c32(inputs["ffn_w_in"][0]),
              "w_out0": c32(inputs["ffn_w_out"][0]), "w_down": c32(inputs["mla_w_down"][0]), "qkn": qkn,
              "w_uq": c32(inputs["mla_w_uq"][0]), "w_ukv": c32(inputs["mla_w_ukv"][0]), "ropec": ropec,
              "w_o1": c32(inputs["mla_w_o"][0]), "w_in1": c32(inputs["ffn_w_in"][1]),
              "w_out1": c32(inputs["ffn_w_out"][1])}
    in_maps = []
    for c in range(NCORE):
        m = dict(shared)
        m["x"] = np.ascontiguousarray(x[perm[c]])
        m["posq"] = np.ascontiguousarray(pos[perm[c]][None, :])
        m["masks"] = masks[c]
        in_maps.append(m)
    res = run(build_fused(), in_maps)
    out = np.empty((S, D), f32)
    for c in range(NCORE):
        out[perm[c]] = res[c]["out"]
    return out[None]


FUSED = False
kernel = kernel_fused if FUSED else kernel_unfused
```
